# Optimizing a Trainium2 kernel written in Bass

```python
import jax, jax.numpy as jnp
from jax import lax
import numpy as np

D_MODEL = 1024
BATCH = 8
SEQ = 2048
DEPTH = 1
DEC_BATCH = 128
DEC_SEQ = 8
PAST_LEN = 16384
PAGE_SIZE = 128

D_A = D_MODEL
D_B = D_MODEL
K_A = 3
K_B = 31
PLE_DIM = 256
EPS = 1e-6
LN_EPS = 1e-5
SPLITS = [D_A, 2 * D_A, 3 * D_A, 4 * D_A,
          4 * D_A + D_B, 4 * D_A + 2 * D_B, 4 * D_A + 3 * D_B,
          4 * D_A + 3 * D_B + D_MODEL]
D_IN = 4 * D_A + 3 * D_B + 2 * D_MODEL

kernel_name = "hybrid_shortconv_conformer_conv_decoder_step"


def _rmsnorm(x, g):
    xf = x.astype(jnp.float32)
    y = xf * lax.rsqrt(jnp.mean(xf * xf, axis=-1, keepdims=True) + EPS)
    return (y * g.astype(jnp.float32)).astype(x.dtype)


def _layernorm(x, g, b):
    xf = x.astype(jnp.float32)
    mu = jnp.mean(xf, axis=-1, keepdims=True)
    xc = xf - mu
    var = jnp.mean(xc * xc, axis=-1, keepdims=True)
    y = xc * lax.rsqrt(var + LN_EPS) * g.astype(jnp.float32) + b.astype(jnp.float32)
    return y.astype(x.dtype)


def _causal_dwconv(x_ext, w):
    c = x_ext.shape[-1]
    return lax.conv_general_dilated(
        x_ext, w[:, None, :].astype(x_ext.dtype), window_strides=(1,), padding='VALID',
        dimension_numbers=('NWC', 'WIO', 'NWC'), feature_group_count=c)


def _layer(h, p, buf_a, buf_b, g_mix, w_in, w_conv_a, w_out_a, w_conv_b, b_conv_b,
           ln_g, ln_b, w_out_b, w_o, w_pe, g_ple, w_pg):
    u = _rmsnorm(h, g_mix)
    proj = jnp.einsum('ntd,de->nte', u, w_in)
    a_b, a_c, a_h, a_z, b_v, b_g, b_z, gate_a, gate_b = jnp.split(proj, SPLITS, axis=-1)

    s = a_c * a_h
    ext_a = jnp.concatenate([buf_a.astype(s.dtype), s], axis=1)
    ya = a_b * _causal_dwconv(ext_a, w_conv_a) * jax.nn.silu(a_z)
    ya = jnp.einsum('ntc,cd->ntd', ya, w_out_a)
    new_buf_a = ext_a[:, -(K_A - 1):]

    v = b_v * jax.nn.sigmoid(b_g)
    ext_b = jnp.concatenate([buf_b.astype(v.dtype), v], axis=1)
    c = _causal_dwconv(ext_b, w_conv_b) + b_conv_b
    yb = jax.nn.silu(_layernorm(c, ln_g, ln_b)) * jax.nn.silu(b_z)
    yb = jnp.einsum('ntc,cd->ntd', yb, w_out_b)
    new_buf_b = ext_b[:, -(K_B - 1):]

    m = jax.nn.sigmoid(gate_a) * ya + jax.nn.sigmoid(gate_b) * yb
    h = h + jnp.einsum('ntd,de->nte', m, w_o)

    pe = jnp.einsum('ntp,pd->ntd', p.astype(h.dtype), w_pe)
    pg = jax.nn.sigmoid(jnp.einsum('ntd,de->nte', _rmsnorm(h, g_ple), w_pg))
    h = h + pg * pe
    return h, new_buf_a, new_buf_b


def setup_inputs(seed: int = 0) -> dict:
    key = jax.random.key(seed)
    ks = jax.random.split(key, 24)
    f32 = jnp.float32

    def nrm(k, shape, scale):
        return jax.random.normal(k, shape, f32) * scale

    return {
        "x_prompt": nrm(ks[0], (BATCH, SEQ, D_MODEL), 1.0),
        "x_sample": nrm(ks[1], (DEC_BATCH, DEC_SEQ, D_MODEL), 1.0),
        "state_conv_a": nrm(ks[2], (DEPTH, DEC_BATCH, K_A - 1, D_A), 1.0),
        "state_conv_b": nrm(ks[3], (DEPTH, DEC_BATCH, K_B - 1, D_B), 0.5),
        "p_prompt": nrm(ks[4], (DEPTH, BATCH, SEQ, PLE_DIM), 1.0),
        "p_sample": nrm(ks[5], (DEPTH, DEC_BATCH, DEC_SEQ, PLE_DIM), 1.0),
        "g_mix": 1.0 + nrm(ks[6], (DEPTH, D_MODEL), 0.02),
        "w_in": nrm(ks[7], (DEPTH, D_MODEL, D_IN), D_MODEL ** -0.5),
        "w_conv_a": nrm(ks[8], (DEPTH, K_A, D_A), K_A ** -0.5),
        "w_out_a": nrm(ks[9], (DEPTH, D_A, D_MODEL), D_A ** -0.5),
        "w_conv_b": nrm(ks[10], (DEPTH, K_B, D_B), K_B ** -0.5),
        "b_conv_b": nrm(ks[11], (DEPTH, D_B), 0.02),
        "ln_g": 1.0 + nrm(ks[12], (DEPTH, D_B), 0.02),
        "ln_b": nrm(ks[13], (DEPTH, D_B), 0.02),
        "w_out_b": nrm(ks[14], (DEPTH, D_B, D_MODEL), D_B ** -0.5),
        "w_o": nrm(ks[15], (DEPTH, D_MODEL, D_MODEL), D_MODEL ** -0.5),
        "w_pe": nrm(ks[16], (DEPTH, PLE_DIM, D_MODEL), PLE_DIM ** -0.5),
        "g_ple": 1.0 + nrm(ks[17], (DEPTH, D_MODEL), 0.02),
        "w_pg": nrm(ks[18], (DEPTH, D_MODEL, D_MODEL), D_MODEL ** -0.5),
        "g_final": 1.0 + nrm(ks[19], (D_MODEL,), 0.02),
    }


def reference(x_prompt, x_sample, state_conv_a, state_conv_b, p_prompt, p_sample,
              g_mix, w_in, w_conv_a, w_out_a, w_conv_b, b_conv_b, ln_g, ln_b,
              w_out_b, w_o, w_pe, g_ple, w_pg, g_final):
    n_p = x_prompt.shape[0]
    hp, hs = x_prompt, x_sample
    na_p, nb_p, na_s, nb_s = [], [], [], []
    for i in range(DEPTH):
        lw = (g_mix[i], w_in[i], w_conv_a[i], w_out_a[i], w_conv_b[i], b_conv_b[i],
              ln_g[i], ln_b[i], w_out_b[i], w_o[i], w_pe[i], g_ple[i], w_pg[i])
        za = jnp.zeros((n_p, K_A - 1, D_A), hp.dtype)
        zb = jnp.zeros((n_p, K_B - 1, D_B), hp.dtype)
        hp, ba, bb = _layer(hp, p_prompt[i], za, zb, *lw)
        na_p.append(ba)
        nb_p.append(bb)
        hs, sa, sb = _layer(hs, p_sample[i], state_conv_a[i], state_conv_b[i], *lw)
        na_s.append(sa)
        nb_s.append(sb)
    y_prompt = _rmsnorm(hp, g_final)
    y_sample = _rmsnorm(hs, g_final)
    return (y_prompt, y_sample, jnp.stack(na_p), jnp.stack(nb_p), jnp.stack(na_s), jnp.stack(nb_s))
```

```python
import contextlib
import numpy as np
import concourse.bass as bass
import concourse.mybir as mybir
from concourse.bass_utils import run_bass_kernel_spmd

F32 = mybir.dt.float32
BF16 = mybir.dt.bfloat16
AF = mybir.ActivationFunctionType
ALU = mybir.AluOpType

D = 1024
NCH = 8
KB = 31
EPS = 1e-6
LN_EPS = 1e-5
NTOK = 2176
NTB = 768

V_GMIX, V_GPLE, V_BCONV, V_LNG, V_LNB = 0, 8, 16, 24, 32
V_WCA = 40
V_WCB = 64
V_MASK = V_WCB + 256
NV = V_MASK + 64

BLOCKS = [
    dict(p0=0, npr=768, sbs=[(0, 512, "p"), (512, 256, "p")]),
    dict(p0=768, npr=768, sbs=[(0, 512, "p"), (512, 256, "p")]),
    dict(p0=1536, npr=512, sbs=[(0, 512, "p"), (512, 128, "s")]),
]


class Sched:
    def __init__(self, nc):
        self.nc = nc
        self.ops = []
        self.last_writer = {}
        self.readers = {}
        self.chan_count = {}

    def add(self, eng, fn, reads=(), writes=(), dma=None):
        idx = len(self.ops)
        chan = eng if dma is None else ("dma", dma)
        deps = set()
        raw = set()
        for r in reads:
            w = self.last_writer.get(r)
            if w is not None:
                deps.add(w)
                raw.add(w)
        for r in writes:
            w = self.last_writer.get(r)
            if w is not None:
                deps.add(w)
            for rd in self.readers.get(r, {}).values():
                deps.add(rd)
        keep = set()
        for d in deps:
            o = self.ops[d]
            if o["chan"] == eng and dma is None:
                if eng == "pe":
                    continue
            keep.add(d)
        for r in reads:
            self.readers.setdefault(r, {})[chan] = idx
        for r in writes:
            self.last_writer[r] = idx
            self.readers[r] = {}
        seq = None
        if dma is not None:
            seq = self.chan_count.get(chan, 0) + 1
            self.chan_count[chan] = seq
        self.ops.append(dict(eng=eng, chan=chan, fn=fn, deps=keep, dma=dma, seq=seq,
                             signal=dma is not None))
        for d in keep:
            self.ops[d]["signal"] = True
        return idx

    def emit(self, final_wait_eng="sp"):
        nc = self.nc
        ops = self.ops
        cnt = {}
        for o in ops:
            if o["dma"] is None and o["signal"]:
                cnt[o["chan"]] = cnt.get(o["chan"], 0) + 1
                o["seq"] = cnt[o["chan"]]
        chans = []
        for o in ops:
            if o["signal"] and o["chan"] not in chans:
                chans.append(o["chan"])
        with contextlib.ExitStack() as st:
            sems = {}
            for c in chans:
                nm = "s_" + (c if isinstance(c, str) else "d_" + str(c[1]))
                sems[c] = st.enter_context(nc.semaphore(nm))
            block = st.enter_context(nc.Block())
            final = {}
            for o in ops:
                if o["dma"] is not None:
                    final[o["chan"]] = o["seq"] * 16

            def run(engname, e):
                waited = {}
                for o in ops:
                    if o["eng"] != engname:
                        continue
                    need = {}
                    for d in o["deps"]:
                        od = ops[d]
                        v = od["seq"] * (16 if od["dma"] is not None else 1)
                        if v > need.get(od["chan"], 0):
                            need[od["chan"]] = v
                    for c, v in need.items():
                        if waited.get(c, 0) >= v:
                            continue
                        e.wait_ge(sems[c], v)
                        waited[c] = v
                    ins = o["fn"](e)
                    if o["signal"]:
                        ins.then_inc(sems[o["chan"]], 16 if o["dma"] is not None else 1)
                if engname == final_wait_eng:
                    for c, v in final.items():
                        if waited.get(c, 0) < v:
                            e.wait_ge(sems[c], v)

            @block.tensor
            def _(e):
                run("pe", e)

            @block.scalar
            def _(e):
                run("act", e)

            @block.vector
            def _(e):
                run("dve", e)

            @block.gpsimd
            def _(e):
                run("pool", e)

            @block.sync
            def _(e):
                run("sp", e)


def build_program():
    nc = bass.Bass("TRN2", target_bir_lowering=False)

    def din(name, shape):
        return nc.dram_tensor(name, shape, F32, kind="ExternalInput").ap()

    def dout(name, shape):
        return nc.dram_tensor(name, shape, F32, kind="ExternalOutput").ap()

    x_d = din("x", [NTOK, D])
    p_d = din("p", [NTOK, 256])
    sa_d = din("sa", [32, D])
    sb_d = din("sb", [480, D])
    w_in = din("w_in", [D, 9 * D])
    w_oa = din("w_oa", [D, D])
    w_ob = din("w_ob", [D, D])
    w_o = din("w_o", [D, D])
    w_pg = din("w_pg", [D, D])
    w_pe = din("w_pe", [256, D])
    vecs_d = din("vecs", [128, NV])
    gfin_d = din("gfin", [128, D])
    ident_d = din("ident", [128, 128])

    y_d = dout("y", [NTOK, D])
    ncap_d = dout("nca_p", [2, D])
    ncbp_d = dout("ncb_p", [30, D])
    ncas_d = dout("nca_s", [32, D])
    ncbs_new_d = dout("ncb_s_new", [128, D])
    ncbs_hist_d = dout("ncb_s_hist", [352, D])

    with contextlib.ExitStack() as st:
        def sb(name, shape, dt):
            return st.enter_context(nc.sbuf_tensor("t_" + name, shape, dt))

        idf = sb("idf", [128, 128], F32)
        idb = sb("idb", [128, 128], BF16)
        onesf = sb("onesf", [128, 128], F32)
        vecs = sb("vecs", [128, NV], F32)
        gfin = sb("gfin", [128, D], F32)
        mhalf = sb("mhalf", [128, 1], F32)
        histb = sb("histb", [128, NCH, 480], BF16)
        hista = sb("hista", [128, NCH, 32], F32)
        halo_v = sb("halo_v", [128, NCH, 30], BF16)
        halo_s = sb("halo_s", [128, NCH, 2], F32)
        uT = sb("uT", [128, NCH, NTB], BF16)
        yaT = sb("yaT", [128, NCH, NTB], BF16)
        cbuf = sb("cbuf", [128, NCH, NTB], F32)
        cbufb = cbuf.bitcast(BF16)
        NRING = 8
        ring = [sb(f"ring{i}", [128, NCH, 128], BF16) for i in range(NRING)]
        wo_t = sb("wo_t", [128, NCH, D], BF16)
        wpg_t = sb("wpg_t", [128, NCH, D], BF16)
        wpe_t = sb("wpe_t", [128, 2, D], BF16)
        Wp = [sb(f"Wp{i}", [128, 32, 64], BF16) for i in range(2)]
        RL = 608
        NRS = 3
        Rb = [[sb(f"R{i}_{g}", [128, RL], BF16) for g in range(2)] for i in range(NRS)]
        vext = [sb(f"vext{i}", [128, 30 + NTB + 2], BF16) for i in range(2)]
        vexts = [sb(f"vexts{i}", [128, 624], BF16) for i in range(2)]
        sext = [sb(f"sext{i}", [128, 2 + NTB], F32) for i in range(1)]
        sexts = [sb(f"sexts{i}", [128, 160], F32) for i in range(1)]
        xs = [sb(f"xs{i}", [128, D], F32) for i in range(3)]
        xb = [sb(f"xb{i}", [128, D], BF16) for i in range(2)]
        ssq = sb("ssq", [128, 64], F32)
        msq = sb("msq", [128, 64], F32)
        rsd = sb("rsd", [128, 64], F32)
        NTMP = 6
        tmp = [sb(f"tmp{i}", [128, 512], F32) for i in range(NTMP)]
        accs = sb("accs", [128, NTB], F32)
        accq = sb("accq", [128, NTB], F32)
        rstdB = sb("rstdB", [128, NTB], F32)
        nmrB = sb("nmrB", [128, NTB], F32)
        Abuf = [sb(f"Abuf{i}", [128, D], F32) for i in range(2)]
        Bbuf = sb("Bbuf", [128, D], F32)
        yt = sb("yt", [128, D], F32)
        u2T = sb("u2T", [128, NCH, 128], BF16)
        pbt = [sb(f"pbt{i}", [128, 256], BF16) for i in range(2)]
        pT = sb("pT", [128, 2, 128], BF16)
        vkeep = sb("vkeep", [128, NCH, 32], F32)
        skeep = sb("skeep", [128, NCH, 2], F32)
        vkeep_s = sb("vkeep_s", [128, NCH, 128], F32)
        BB = [Bbuf[:, :], vkeep_s.rearrange("p a b -> p (a b)")]
        skeep_s = sb("skeep_s", [128, NCH, 32], F32)
        ps = st.enter_context(nc.psum_tensor("ps", [128, 8, 512], F32))
        psb = ps.bitcast(BF16)

        S = Sched(nc)

        free_banks = list(range(8))

        def balloc():
            return free_banks.pop(0)

        def bfree(b):
            free_banks.append(b)

        tmp_ctr = [0]

        def talloc():
            k = tmp_ctr[0] % NTMP
            tmp_ctr[0] += 1
            return k

        ring_ctr = [0]

        SWQ = 4
        swq_ctr = [0]

        def swq():
            r = f"swq{swq_ctr[0] % SWQ}"
            swq_ctr[0] += 1
            return r

        WL, WG = [], []

        def add_group(srcs):
            WG.append(list(range(len(WL), len(WL) + len(srcs))))
            for sap in srcs:
                WL.append((sap, len(WG) - 1))

        def wcol(c0, j):
            return w_in[:, c0 * D + 128 * j:c0 * D + 128 * j + 128]

        for _bi in range(len(BLOCKS)):
            for j in range(NCH):
                add_group([wcol(4, j), wcol(5, j)])
            for j in range(NCH):
                add_group([wcol(1, j), wcol(2, j), wcol(0, j), wcol(3, j)])
                if j >= 4:
                    add_group([wcol(6, 2 * (j - 4))])
                    add_group([wcol(6, 2 * (j - 4) + 1)])
            for j in range(NCH):
                add_group([w_oa[:, 128 * j:128 * j + 128], w_ob[:, 128 * j:128 * j + 128], wcol(7, j), wcol(8, j)])
        wst = dict(nl=0, ng=0, done=set())

        def pump(maxn=4):
            nem = 0
            while nem < maxn and wst["nl"] < len(WL) and (wst["nl"] < NRING or WL[wst["nl"] - NRING][1] in wst["done"]):
                nem += 1
                t = wst["nl"]
                wst["nl"] += 1
                k = t % NRING
                src = WL[t][0]
                S.add("pool", lambda e, k=k, src=src: e.dma_start(
                    out=ring[k][:, :, :], in_=src.rearrange("(k p) m -> p k m", p=128)),
                    writes=[f"ring{k}", swq()], dma=f"ring{k}")

        def wgroup():
            g = wst["ng"]
            wst["ng"] += 1
            if not all(t < wst["nl"] for t in WG[g]):
                pump(NRING)
            assert all(t < wst["nl"] for t in WG[g]), "weight tile not prefetched"
            return g, [t % NRING for t in WG[g]]

        def wdone(g):
            wst["done"].add(g)
            pump()

        stat_ctr = [0]

        def rms_stats(src_ap, src_res, junk_ap, junk_res):
            c = stat_ctr[0] % 64
            stat_ctr[0] += 1
            S.add("act", lambda e, c=c: e.activation(out=junk_ap, in_=src_ap, func=AF.Square,
                                                     accum_out=ssq[:, c:c + 1]),
                  reads=src_res, writes=list(junk_res) + [f"ssq{c}"])
            S.add("pool", lambda e, c=c: e.tensor_scalar(out=msq[:, c:c + 1], in0=ssq[:, c:c + 1],
                                                         scalar1=1.0 / D, scalar2=EPS,
                                                         op0=ALU.mult, op1=ALU.add),
                  reads=[f"ssq{c}"], writes=[f"msq{c}"])
            S.add("pool", lambda e, c=c: e.tensor_tensor(out=rsd[:, c:c + 1], in0=msq[:, c:c + 1],
                                                         in1=mhalf[:, 0:1], op=ALU.pow),
                  reads=[f"msq{c}", "mhalf"], writes=[f"rsd{c}"])
            return rsd[:, c:c + 1], f"rsd{c}"

        def vcol(off, j):
            return vecs[:, off + j:off + j + 1]

        S.add("sp", lambda e: e.dma_start(out=idf[:], in_=ident_d), writes=["idf"], dma="idf")
        S.add("sp", lambda e: e.dma_start(out=vecs[:], in_=vecs_d), writes=["vecs"], dma="vecs")
        S.add("sp", lambda e: e.dma_start(out=gfin[:], in_=gfin_d), writes=["gfin"], dma="gfin")
        S.add("dve", lambda e: e.tensor_copy(out=idb[:], in_=idf[:]), reads=["idf"], writes=["idb"])
        S.add("pool", lambda e: e.memset(onesf[:], 1.0), writes=["onesf"])
        for i in range(2):
            S.add("pool", lambda e, i=i: e.memset(vext[i][:], 0.0), writes=[f"vext{i}h", f"vext{i}_0", f"vext{i}_1"])
            S.add("pool", lambda e, i=i: e.memset(vexts[i][:], 0.0), writes=[f"vexts{i}h", f"vexts{i}n"])
        S.add("pool", lambda e: e.memset(mhalf[:], -0.5), writes=["mhalf"])
        S.add("pool", lambda e: e.memset(halo_v[:], 0.0), writes=[f"halo_v{j}" for j in range(NCH)])
        S.add("pool", lambda e: e.memset(halo_s[:], 0.0), writes=[f"halo_s{j}" for j in range(NCH)])
        pump()
        RESIDENT = []
        for h in range(2):
            RESIDENT.append(lambda h=h: S.add("pool", lambda e: e.dma_start(
                out=wo_t[:, 4 * h:4 * h + 4, :],
                in_=w_o[512 * h:512 * h + 512, :].rearrange("(k p) m -> p k m", p=128)),
                writes=[f"wo{h}", swq()], dma=f"wo{h}"))
        for h in range(2):
            RESIDENT.append(lambda h=h: S.add("pool", lambda e: e.dma_start(
                out=wpg_t[:, 4 * h:4 * h + 4, :],
                in_=w_pg[512 * h:512 * h + 512, :].rearrange("(k p) m -> p k m", p=128)),
                writes=[f"wpg{h}", swq()], dma=f"wpg{h}"))
        RESIDENT.append(lambda: S.add("pool", lambda e: e.dma_start(
            out=wpe_t[:], in_=w_pe.rearrange("(k p) m -> p k m", p=128)),
            writes=["wpe", swq()], dma="wpe"))

        def hist_setup():
            rows_list = [(0, 128), (128, 128), (256, 128), (384, 96)]
            for ti, (r0, nr) in enumerate(rows_list):
                sl = ti % 2
                S.add("sp", lambda e, sl=sl, r0=r0, nr=nr: e.dma_start(out=Abuf[sl][0:nr, :], in_=sb_d[r0:r0 + nr, :]),
                      writes=[f"A{sl}_0", f"A{sl}_1"], dma=f"hst{sl}")
                b0, b1 = balloc(), balloc()
                for k in range(NCH):
                    b = b0 if k < 4 else b1
                    S.add("pe", lambda e, sl=sl, k=k, b=b, nr=nr: e.transpose(
                        out=ps[:, b, (k % 4) * 128:(k % 4) * 128 + nr], in_=Abuf[sl][0:nr, k * 128:(k + 1) * 128],
                        identity=idf[0:nr, 0:nr]),
                        reads=[f"A{sl}_0", f"A{sl}_1", "idf"], writes=[f"ps{b}"])
                for h, b in enumerate((b0, b1)):
                    S.add("act", lambda e, h=h, b=b, r0=r0, nr=nr: e.activation(
                        out=histb[:, 4 * h:4 * h + 4, r0:r0 + nr],
                        in_=ps[:, b, :].rearrange("p (k t) -> p k t", t=128)[:, :, 0:nr], func=AF.Copy),
                        reads=[f"ps{b}"], writes=[f"histb{ti}_{h}"])
                bfree(b0)
                bfree(b1)
            S.add("sp", lambda e: e.dma_start(out=Abuf[0][0:32, :], in_=sa_d), writes=["A0_0", "A0_1"], dma="hst0")
            b0 = balloc()
            for k in range(NCH):
                S.add("pe", lambda e, k=k, b0=b0: e.transpose(out=ps[:, b0, k * 32:(k + 1) * 32],
                                                              in_=Abuf[0][0:32, k * 128:(k + 1) * 128],
                                                              identity=idf[0:32, 0:32]),
                      reads=["A0_0", "A0_1", "idf"], writes=[f"ps{b0}"])
            S.add("act", lambda e, b0=b0: e.activation(out=hista[:], in_=ps[:, b0, 0:256].rearrange("p (k t) -> p k t", t=32),
                                                       func=AF.Copy),
                  reads=[f"ps{b0}"], writes=["hista"])
            bfree(b0)
            S.add("sp", lambda e: e.dma_start(out=ncbs_hist_d, in_=sb_d[128:480, :]), dma="ncbs_hist")

        HISTB_RES = [f"histb{ti}_{h}" for ti in range(4) for h in range(2)]

        def tiles_of(blk):
            t = [(blk["p0"] + 128 * i, 128 * i) for i in range(blk["npr"] // 128)]
            if any(k == "s" for _, _, k in blk["sbs"]):
                t.append((2048, blk["npr"]))
            return t

        s1st = {}

        s1_loaded = set()

        def s1_load(tl, ti):
            s1_loaded.add((tl[0][0], ti))
            row0, off = tl[ti]
            xsl = (ti + 2) % 3
            S.add("sp", lambda e, xsl=xsl, row0=row0: e.dma_start(out=xs[xsl][:], in_=x_d[row0:row0 + 128, :]),
                  writes=[f"xs{xsl}"], dma=f"xs{xsl}")

        def s1_stats(tl, ti):
            xsl = (ti + 2) % 3
            sl = ti % 2
            s1st[ti] = rms_stats(xs[xsl][:], [f"xs{xsl}"], xb[sl][:], [f"xb{sl}"])

        def s1_rest(tl, ti):
            pump(2)
            row0, off = tl[ti]
            xsl, sl = (ti + 2) % 3, ti % 2
            rs_ap, rs_res = s1st[ti]
            S.add("act", lambda e, sl=sl, xsl=xsl, rs_ap=rs_ap: e.activation(out=xb[sl][:], in_=xs[xsl][:], func=AF.Copy,
                                                                             scale=rs_ap),
                  reads=[f"xs{xsl}", rs_res], writes=[f"xb{sl}"])
            b = balloc()
            for k in range(NCH):
                S.add("pe", lambda e, sl=sl, k=k, b=b: e.transpose(out=psb[:, b, k * 128:(k + 1) * 128],
                                                                    in_=xb[sl][:, k * 128:(k + 1) * 128],
                                                                    identity=idb[:]),
                      reads=[f"xb{sl}", "idb"], writes=[f"ps{b}"])
            S.add("dve", lambda e, b=b, off=off: e.tensor_tensor(
                out=uT[:, :, off:off + 128], in0=psb[:, b, :].rearrange("p (k t) -> p k t", t=128),
                in1=vecs[:, V_GMIX:V_GMIX + 8].unsqueeze(2).to_broadcast([128, NCH, 128]), op=ALU.mult),
                reads=[f"ps{b}", "vecs"], writes=[f"uT{off // 128}"])
            bfree(b)

        def out_T(src_fn, nrow, res_list, stage, stage_res, dsts, extra_reads=()):
            b0, b1 = balloc(), balloc()
            for j in range(NCH):
                b = b0 if j < 4 else b1
                S.add("pe", lambda e, j=j, b=b: e.transpose(out=ps[0:nrow, b, (j % 4) * 128:(j % 4) * 128 + 128],
                                                             in_=src_fn(j), identity=idf[:]),
                      reads=[res_list[j], "idf"] + list(extra_reads), writes=[f"ps{b}"])
            for h, b in enumerate((b0, b1)):
                S.add("act", lambda e, h=h, b=b: e.activation(out=stage[0:nrow, 512 * h:512 * h + 512], in_=ps[0:nrow, b, :],
                                                              func=AF.Copy),
                      reads=[f"ps{b}"], writes=[stage_res[h]])
            bfree(b0)
            bfree(b1)
            for (dst, r0, r1, key) in dsts:
                S.add("sp", lambda e, dst=dst, r0=r0, r1=r1: e.dma_start(out=dst, in_=stage[r0:r1, :]),
                      reads=stage_res, dma=key)

        for bi, blk in enumerate(BLOCKS):
            p0, npr, sbs = blk["p0"], blk["npr"], blk["sbs"]
            last = bi == len(BLOCKS) - 1
            has_s = any(k == "s" for _, _, k in sbs)
            tiles = tiles_of(blk)
            next_tiles = tiles_of(BLOCKS[bi + 1]) if not last else []

            def ures(off, n):
                return [f"uT{t}" for t in range(off // 128, (off + n) // 128)]

            for ti in range(min(3, len(tiles))):
                if (tiles[0][0], ti) not in s1_loaded:
                    s1_load(tiles, ti)
            for ti in range(len(tiles) + 1):
                if ti < len(tiles):
                    s1_stats(tiles, ti)
                if ti >= 1:
                    s1_rest(tiles, ti - 1)
                    if ti + 2 < len(tiles):
                        s1_load(tiles, ti + 2)

            jobs = [(j, si) for j in range(NCH) for si in range(len(sbs))]
            s2state = {}
            s2rs = {}
            s2_jobctr = [0]

            def diag_build(j):
                ds = j % 2
                S.add("dve", lambda e, ds=ds, j=j: e.tensor_tensor(
                    out=Wp[ds][:], in0=vecs[:, V_MASK:V_MASK + 64].unsqueeze(1).to_broadcast([128, 32, 64]),
                    in1=vecs[:, V_WCB + 32 * j:V_WCB + 32 * j + 32].unsqueeze(2).to_broadcast([128, 32, 64]),
                    op=ALU.mult), reads=["vecs"], writes=[f"diag{ds}"])

            def s2_proj(j, si):
                off, n, kind = sbs[si]
                if si == 0:
                    wg_, (kv, kg) = wgroup()
                    if bi == 0 and j < len(RESIDENT):
                        RESIDENT[j]()
                    ds = j % 2
                    if j < 2:
                        diag_build(j)
                    vs = j % 2
                    S.add("pool", lambda e, vs=vs, j=j: e.tensor_copy(out=vext[vs][:, 0:30], in_=halo_v[:, j, :]),
                          reads=[f"halo_v{j}"], writes=[f"vext{vs}h"])
                    if has_s:
                        S.add("pool", lambda e, vs=vs, j=j: e.tensor_copy(out=vexts[vs][:, 0:480], in_=histb[:, j, :]),
                              reads=HISTB_RES, writes=[f"vexts{vs}h"])
                    s2state[j] = (kv, kg, ds, vs, wg_)
                kv, kg, ds, vs, wg_ = s2state[j]
                bv, bg = balloc(), balloc()
                first = True
                for (b, kw) in ((bv, kv), (bg, kg)):
                    for k in range(NCH):
                        xr = [f"ring{kv}", f"ring{kg}"] if first else []
                        xw = [f"ps{bv}", f"ps{bg}"] if first else []
                        first = False
                        S.add("pe", lambda e, b=b, kw=kw, k=k, off=off, n=n: e.matmul(
                            ps[:, b, 0:n], lhsT=ring[kw][:, k, :], rhs=uT[:, k, off:off + n],
                            start=(k == 0), stop=(k == NCH - 1)),
                            reads=[f"ring{kw}"] + ures(off, n) + xr, writes=[f"ps{b}"] + xw)
                if si == len(sbs) - 1:
                    wdone(wg_)
                tg = talloc()
                S.add("act", lambda e, bg=bg, tg=tg, n=n: e.activation(out=tmp[tg][:, 0:n], in_=ps[:, bg, 0:n],
                                                                       func=AF.Sigmoid),
                      reads=[f"ps{bg}"], writes=[f"tmp{tg}"])
                if kind == "p":
                    vdst, vres = vext[vs][:, 30 + off:30 + off + n], f"vext{vs}_{si}"
                else:
                    vdst, vres = vexts[vs][:, 480:608], f"vexts{vs}n"
                S.add("dve", lambda e, bv=bv, tg=tg, n=n, vdst=vdst: e.tensor_tensor(
                    out=vdst, in0=ps[:, bv, 0:n], in1=tmp[tg][:, 0:n], op=ALU.mult),
                    reads=[f"ps{bv}", f"tmp{tg}"], writes=[vres])
                rs = s2_jobctr[0] % NRS
                s2_jobctr[0] += 1
                s2rs[(j, si)] = rs
                if kind == "p":
                    rr = [f"vext{vs}h"] + [f"vext{vs}_{q}" for q in range(si + 1)]
                    if si + 1 < len(sbs) and sbs[si + 1][2] == "p":
                        rr.append(f"vext{vs}_{si + 1}")
                else:
                    rr = [f"vexts{vs}h", f"vexts{vs}n"]
                for g in range(2):
                    for jj in range(2):
                        if kind == "p":
                            src = vext[vs][64 * g:64 * g + 64, off + jj:off + jj + n + 30]
                            dst = Rb[rs][g][64 * jj:64 * jj + 64, 0:n + 30]
                        else:
                            src = vexts[vs][64 * g:64 * g + 64, 16 * jj:16 * jj + 608]
                            dst = Rb[rs][g][64 * jj:64 * jj + 64, 0:608]
                        S.add("sp", lambda e, src=src, dst=dst: e.dma_start(out=dst, in_=src),
                              reads=rr, writes=[f"R{rs}_{g}_{jj}"], dma=f"R{rs}_{g}_{jj}")
                if last:
                    if kind == "p":
                        S.add("dve", lambda e, bv=bv, tg=tg, j=j: e.tensor_tensor(
                            out=vkeep[:, j, :], in0=ps[:, bv, 480:512], in1=tmp[tg][:, 480:512], op=ALU.mult),
                            reads=[f"ps{bv}", f"tmp{tg}"], writes=[f"vkeep{j}"])
                    else:
                        S.add("dve", lambda e, bv=bv, tg=tg, j=j: e.tensor_tensor(
                            out=vkeep_s[:, j, :], in0=ps[:, bv, 0:128], in1=tmp[tg][:, 0:128], op=ALU.mult),
                            reads=[f"ps{bv}", f"tmp{tg}"], writes=[f"vkeep_s{j}", "BB1_0", "BB1_1"])
                bfree(bv)
                bfree(bg)

            def s2_conv(j, si):
                off, n, kind = sbs[si]
                kv, kg, ds, vs, wg_ = s2state[j]
                bc = balloc()
                if kind == "p":
                    rres = [f"vext{vs}h"] + [f"vext{vs}_{q}" for q in range(si + 1)]
                else:
                    rres = [f"vexts{vs}h", f"vexts{vs}n"]
                rs = s2rs[(j, si)]
                for i in range(16):
                    for g in range(2):
                        if kind == "p":
                            rhs = Rb[rs][g][:, 2 * i:2 * i + n]
                        else:
                            rhs = Rb[rs][g][:, 32 * i:32 * i + 128]
                        S.add("pe", lambda e, bc=bc, ds=ds, i=i, g=g, rhs=rhs, n=n: e.matmul(
                            ps[64 * g:64 * g + 64, bc, 0:n], lhsT=Wp[ds][:, 16 * g + i, :], rhs=rhs,
                            start=(i == 0), stop=(i == 15), tile_position=(0, 64 * g)),
                            reads=[f"diag{ds}"] + [f"R{rs}_{g}_{jj}" for jj in range(2)], writes=[f"ps{bc}"])
                if si == len(sbs) - 1 and j + 2 < NCH:
                    diag_build(j + 2)
                cres = f"c{j}_{si}"
                S.add("act", lambda e, bc=bc, j=j, off=off, n=n: e.activation(
                    out=cbuf[:, j, off:off + n], in_=ps[:, bc, 0:n], func=AF.Identity, bias=vcol(V_BCONV, j)),
                    reads=[f"ps{bc}", "vecs"], writes=[cres, f"yb{j}_{si}", f"m{j}_{si}"])
                tq = talloc()
                S.add("act", lambda e, bc=bc, j=j, n=n, tq=tq: e.activation(
                    out=tmp[tq][:, 0:n], in_=ps[:, bc, 0:n], func=AF.Square, bias=vcol(V_BCONV, j)),
                    reads=[f"ps{bc}", "vecs"], writes=[f"tmp{tq}"])
                bfree(bc)
                if j == 0:
                    S.add("dve", lambda e, off=off, n=n: e.tensor_copy(out=accs[:, off:off + n], in_=cbuf[:, 0, off:off + n]),
                          reads=[cres], writes=[f"accs{si}"])
                    S.add("dve", lambda e, off=off, n=n, tq=tq: e.tensor_copy(out=accq[:, off:off + n], in_=tmp[tq][:, 0:n]),
                          reads=[f"tmp{tq}"], writes=[f"accq{si}"])
                else:
                    S.add("dve", lambda e, j=j, off=off, n=n: e.tensor_tensor(
                        out=accs[:, off:off + n], in0=accs[:, off:off + n], in1=cbuf[:, j, off:off + n], op=ALU.add),
                        reads=[cres, f"accs{si}"], writes=[f"accs{si}"])
                    S.add("dve", lambda e, off=off, n=n, tq=tq: e.tensor_tensor(
                        out=accq[:, off:off + n], in0=accq[:, off:off + n], in1=tmp[tq][:, 0:n], op=ALU.add),
                        reads=[f"tmp{tq}", f"accq{si}"], writes=[f"accq{si}"])
                nps = sum(1 for _, _, kk in sbs if kk == "p")
                if kind == "p" and si == nps - 1 and not last:
                    S.add("pool", lambda e, vs=vs, j=j, npr=npr: e.tensor_copy(out=halo_v[:, j, :], in_=vext[vs][:, npr:npr + 30]),
                          reads=[f"vext{vs}_{q}" for q in range(nps)], writes=[f"halo_v{j}"])

            def ln_stats():
                for si, (off, n, kind) in enumerate(sbs):
                    b1, b2 = balloc(), balloc()
                    S.add("pe", lambda e, b1=b1, off=off, n=n: e.matmul(ps[:, b1, 0:n], lhsT=onesf[:], rhs=accs[:, off:off + n],
                                                                         start=True, stop=True),
                          reads=["onesf", f"accs{si}"], writes=[f"ps{b1}"])
                    S.add("pe", lambda e, b2=b2, off=off, n=n: e.matmul(ps[:, b2, 0:n], lhsT=onesf[:], rhs=accq[:, off:off + n],
                                                                         start=True, stop=True),
                          reads=["onesf", f"accq{si}"], writes=[f"ps{b2}"])
                    tm, tv = talloc(), talloc()
                    S.add("dve", lambda e, b1=b1, tm=tm, n=n: e.tensor_scalar(
                        out=tmp[tm][:, 0:n], in0=ps[:, b1, 0:n], scalar1=1.0 / D, scalar2=None, op0=ALU.mult),
                        reads=[f"ps{b1}"], writes=[f"tmp{tm}"])
                    S.add("dve", lambda e, tm=tm, tv=tv, n=n: e.tensor_tensor(
                        out=tmp[tv][:, 0:n], in0=tmp[tm][:, 0:n], in1=tmp[tm][:, 0:n], op=ALU.mult),
                        reads=[f"tmp{tm}"], writes=[f"tmp{tv}"])
                    S.add("dve", lambda e, b2=b2, tv=tv, n=n: e.scalar_tensor_tensor(
                        out=tmp[tv][:, 0:n], in0=ps[:, b2, 0:n], scalar=1.0 / D, in1=tmp[tv][:, 0:n],
                        op0=ALU.mult, op1=ALU.subtract),
                        reads=[f"ps{b2}", f"tmp{tv}"], writes=[f"tmp{tv}"])
                    S.add("dve", lambda e, tv=tv, n=n: e.tensor_scalar(
                        out=tmp[tv][:, 0:n], in0=tmp[tv][:, 0:n], scalar1=LN_EPS, scalar2=None, op0=ALU.add),
                        reads=[f"tmp{tv}"], writes=[f"tmp{tv}"])
                    S.add("act", lambda e, tv=tv, n=n: e.activation(out=tmp[tv][:, 0:n], in_=tmp[tv][:, 0:n], func=AF.Ln),
                          reads=[f"tmp{tv}"], writes=[f"tmp{tv}"])
                    S.add("act", lambda e, tv=tv, off=off, n=n: e.activation(
                        out=rstdB[:, off:off + n], in_=tmp[tv][:, 0:n], func=AF.Exp, scale=-0.5),
                        reads=[f"tmp{tv}"], writes=[f"rstdB{si}"])
                    S.add("dve", lambda e, tm=tm, off=off, n=n: e.scalar_tensor_tensor(
                        out=nmrB[:, off:off + n], in0=tmp[tm][:, 0:n], scalar=-1.0, in1=rstdB[:, off:off + n],
                        op0=ALU.mult, op1=ALU.mult),
                        reads=[f"tmp{tm}", f"rstdB{si}"], writes=[f"nmrB{si}"])
                    bfree(b1)
                    bfree(b2)

            def s4_chunk(j):
                wg4, (kz,) = wgroup()
                for si, (off, n, kind) in enumerate(sbs):
                    pz = balloc()
                    for k in range(NCH):
                        S.add("pe", lambda e, pz=pz, kz=kz, k=k, off=off, n=n: e.matmul(
                            ps[:, pz, 0:n], lhsT=ring[kz][:, k, :], rhs=uT[:, k, off:off + n],
                            start=(k == 0), stop=(k == NCH - 1)),
                            reads=[f"ring{kz}"] + ures(off, n), writes=[f"ps{pz}"])
                    if si == len(sbs) - 1:
                        wdone(wg4)
                    tx = talloc()
                    S.add("dve", lambda e, tx=tx, j=j, off=off, n=n: e.tensor_tensor(
                        out=tmp[tx][:, 0:n], in0=cbuf[:, j, off:off + n], in1=rstdB[:, off:off + n], op=ALU.mult),
                        reads=[f"c{j}_{si}", f"rstdB{si}"], writes=[f"tmp{tx}"])
                    S.add("dve", lambda e, tx=tx, off=off, n=n: e.tensor_tensor(
                        out=tmp[tx][:, 0:n], in0=tmp[tx][:, 0:n], in1=nmrB[:, off:off + n], op=ALU.add),
                        reads=[f"tmp{tx}", f"nmrB{si}"], writes=[f"tmp{tx}"])
                    tl = talloc()
                    S.add("act", lambda e, tx=tx, tl=tl, j=j, n=n: e.activation(
                        out=tmp[tl][:, 0:n], in_=tmp[tx][:, 0:n], func=AF.Silu, scale=vcol(V_LNG, j), bias=vcol(V_LNB, j)),
                        reads=[f"tmp{tx}", "vecs"], writes=[f"tmp{tl}"])
                    tb = talloc()
                    S.add("act", lambda e, pz=pz, tb=tb, n=n: e.activation(out=tmp[tb][:, 0:n], in_=ps[:, pz, 0:n], func=AF.Silu),
                          reads=[f"ps{pz}"], writes=[f"tmp{tb}"])
                    S.add("dve", lambda e, tl=tl, tb=tb, j=j, off=off, n=n: e.tensor_tensor(
                        out=cbufb[:, j, 2 * off:2 * off + n], in0=tmp[tl][:, 0:n], in1=tmp[tb][:, 0:n], op=ALU.mult),
                        reads=[f"tmp{tl}", f"tmp{tb}"], writes=[f"yb{j}_{si}", f"c{j}_{si}"])
                    bfree(pz)

            def s3_chunk(j):
                if j == 3:
                    ln_stats()
                wg3, (kc, kh, kb_, kz) = wgroup()
                ss_ = 0
                S.add("pool", lambda e, ss_=ss_, j=j: e.tensor_copy(out=sext[ss_][:, 0:2], in_=halo_s[:, j, :]),
                      reads=[f"halo_s{j}"], writes=[f"sext{ss_}h"])
                if has_s:
                    S.add("pool", lambda e, ss_=ss_, j=j: e.tensor_copy(out=sexts[ss_][:, 0:32], in_=hista[:, j, :]),
                          reads=["hista"], writes=[f"sexts{ss_}h"])
                nps = sum(1 for _, _, kk in sbs if kk == "p")
                for si, (off, n, kind) in enumerate(sbs):
                    bks = [balloc() for _ in range(4)]
                    first = True
                    for (b, kw) in zip(bks, (kc, kh, kb_, kz)):
                        for k in range(NCH):
                            xr = [f"ring{q}" for q in (kc, kh, kb_, kz)] if first else []
                            xw = [f"ps{q}" for q in bks] if first else []
                            first = False
                            S.add("pe", lambda e, b=b, kw=kw, k=k, off=off, n=n: e.matmul(
                                ps[:, b, 0:n], lhsT=ring[kw][:, k, :], rhs=uT[:, k, off:off + n],
                                start=(k == 0), stop=(k == NCH - 1)),
                                reads=[f"ring{kw}"] + ures(off, n) + xr, writes=[f"ps{b}"] + xw)
                    pc, ph, pb, pz = bks
                    if si == len(sbs) - 1:
                        wdone(wg3)
                    th = talloc()
                    S.add("act", lambda e, ph=ph, th=th, n=n: e.activation(out=tmp[th][:, 0:n], in_=ps[:, ph, 0:n], func=AF.Copy),
                          reads=[f"ps{ph}"], writes=[f"tmp{th}"])
                    if kind == "p":
                        sdst, sres = sext[ss_][:, 2 + off:2 + off + n], f"sext{ss_}_{si}"
                        taps = [sext[ss_][:, off + k:off + k + n] for k in range(3)]
                        rres = [f"sext{ss_}h"] + [f"sext{ss_}_{q}" for q in range(si + 1)]
                    else:
                        sdst, sres = sexts[ss_][:, 32:160], f"sexts{ss_}n"
                        taps = [sexts[ss_][:, 16 * k:16 * k + 128] for k in range(3)]
                        rres = [f"sexts{ss_}h", f"sexts{ss_}n"]
                    S.add("dve", lambda e, pc=pc, th=th, n=n, sdst=sdst: e.tensor_tensor(
                        out=sdst, in0=ps[:, pc, 0:n], in1=tmp[th][:, 0:n], op=ALU.mult),
                        reads=[f"ps{pc}", f"tmp{th}"], writes=[sres])
                    tz = talloc()
                    S.add("act", lambda e, pz=pz, tz=tz, n=n: e.activation(out=tmp[tz][:, 0:n], in_=ps[:, pz, 0:n], func=AF.Silu),
                          reads=[f"ps{pz}"], writes=[f"tmp{tz}"])
                    S.add("dve", lambda e, pb=pb, tz=tz, n=n: e.tensor_tensor(
                        out=tmp[tz][:, 0:n], in0=ps[:, pb, 0:n], in1=tmp[tz][:, 0:n], op=ALU.mult),
                        reads=[f"ps{pb}", f"tmp{tz}"], writes=[f"tmp{tz}"])
                    for b in bks:
                        bfree(b)
                    t2 = talloc()
                    S.add("dve", lambda e, t2=t2, taps=taps, n=n, j=j: e.tensor_scalar(
                        out=tmp[t2][:, 0:n], in0=taps[0], scalar1=vecs[:, V_WCA + 3 * j:V_WCA + 3 * j + 1], scalar2=None,
                        op0=ALU.mult), reads=rres + ["vecs"], writes=[f"tmp{t2}"])
                    for k in (1, 2):
                        S.add("dve", lambda e, t2=t2, taps=taps, n=n, j=j, k=k: e.scalar_tensor_tensor(
                            out=tmp[t2][:, 0:n], in0=taps[k], scalar=vecs[:, V_WCA + 3 * j + k:V_WCA + 3 * j + k + 1],
                            in1=tmp[t2][:, 0:n], op0=ALU.mult, op1=ALU.add),
                            reads=rres + ["vecs", f"tmp{t2}"], writes=[f"tmp{t2}"])
                    S.add("dve", lambda e, t2=t2, tz=tz, j=j, off=off, n=n: e.tensor_tensor(
                        out=yaT[:, j, off:off + n], in0=tmp[t2][:, 0:n], in1=tmp[tz][:, 0:n], op=ALU.mult),
                        reads=[f"tmp{t2}", f"tmp{tz}"], writes=[f"ya{j}_{si}"])
                    if last:
                        if kind == "p":
                            S.add("pool", lambda e, ss_=ss_, j=j: e.tensor_copy(out=skeep[:, j, :], in_=sext[ss_][:, 512:514]),
                                  reads=[sres], writes=[f"skeep{j}"])
                        else:
                            S.add("pool", lambda e, ss_=ss_, j=j: e.tensor_copy(out=skeep_s[:, j, :], in_=sexts[ss_][:, 128:160]),
                                  reads=[sres], writes=[f"skeep_s{j}"])
                    if kind == "p" and si == nps - 1 and not last:
                        S.add("pool", lambda e, ss_=ss_, j=j, npr=npr: e.tensor_copy(out=halo_s[:, j, :], in_=sext[ss_][:, npr:npr + 2]),
                              reads=[f"sext{ss_}_{q}" for q in range(nps)], writes=[f"halo_s{j}"])

                if j >= 4:
                    s4_chunk(2 * (j - 4))
                    s4_chunk(2 * (j - 4) + 1)

            SKEW = 2
            for idx in range(len(jobs) + SKEW):
                if idx < len(jobs):
                    s2_proj(*jobs[idx])
                if idx == len(jobs):
                    s3_chunk(0)
                if idx >= SKEW:
                    s2_conv(*jobs[idx - SKEW])
            if bi == 0:
                hist_setup()

            for j in range(1, NCH):
                s3_chunk(j)

            if last:
                out_T(lambda j: vkeep_s[:, j, :], 128, [f"vkeep_s{j}" for j in range(NCH)], BB[0], ["BB0_0", "BB0_1"],
                      [(ncbs_new_d, 0, 128, "o_ncbs")], extra_reads=["BB1_0", "BB1_1"])

            if next_tiles:
                s1_load(next_tiles, 0)
            for ti in range(min(2, len(tiles))):
                row0, off = tiles[ti]
                sl = ti % 2
                S.add("sp", lambda e, sl=sl, row0=row0: e.dma_start(out=xs[sl][:], in_=x_d[row0:row0 + 128, :]),
                      writes=[f"xs{sl}"], dma=f"xs{sl}")
                S.add("pool", lambda e, sl=sl, row0=row0: e.dma_start(out=pbt[sl][:], in_=p_d[row0:row0 + 128, :]),
                      writes=[f"pbt{sl}", swq()], dma=f"pbt{sl}")

            for j in range(NCH):
                wg5, (ka, kbb, kga, kgb) = wgroup()
                for si, (off, n, kind) in enumerate(sbs):
                    bks = [balloc() for _ in range(4)]
                    pya, pyb, pga, pgb = bks
                    for k in range(NCH):
                        xr = [f"ring{q}" for q in (ka, kbb, kga, kgb)] if k == 0 else []
                        xw = [f"ps{q}" for q in bks] if k == 0 else []
                        S.add("pe", lambda e, pya=pya, ka=ka, k=k, off=off, n=n: e.matmul(
                            ps[:, pya, 0:n], lhsT=ring[ka][:, k, :], rhs=yaT[:, k, off:off + n],
                            start=(k == 0), stop=(k == NCH - 1)),
                            reads=[f"ring{ka}", f"ya{k}_{si}"] + xr, writes=[f"ps{pya}"] + xw)
                    for (b, kw) in ((pga, kga), (pgb, kgb)):
                        for k in range(NCH):
                            S.add("pe", lambda e, b=b, kw=kw, k=k, off=off, n=n: e.matmul(
                                ps[:, b, 0:n], lhsT=ring[kw][:, k, :], rhs=uT[:, k, off:off + n],
                                start=(k == 0), stop=(k == NCH - 1)),
                                reads=[f"ring{kw}"] + ures(off, n), writes=[f"ps{b}"])
                    for k in range(NCH):
                        S.add("pe", lambda e, pyb=pyb, kbb=kbb, k=k, off=off, n=n: e.matmul(
                            ps[:, pyb, 0:n], lhsT=ring[kbb][:, k, :], rhs=cbufb[:, k, 2 * off:2 * off + n],
                            start=(k == 0), stop=(k == NCH - 1)),
                            reads=[f"ring{kbb}", f"yb{k}_{si}"], writes=[f"ps{pyb}"])
                    if si == len(sbs) - 1:
                        wdone(wg5)
                    ta, tb = talloc(), talloc()
                    S.add("act", lambda e, pga=pga, ta=ta, n=n: e.activation(out=tmp[ta][:, 0:n], in_=ps[:, pga, 0:n], func=AF.Sigmoid),
                          reads=[f"ps{pga}"], writes=[f"tmp{ta}"])
                    S.add("act", lambda e, pgb=pgb, tb=tb, n=n: e.activation(out=tmp[tb][:, 0:n], in_=ps[:, pgb, 0:n], func=AF.Sigmoid),
                          reads=[f"ps{pgb}"], writes=[f"tmp{tb}"])
                    S.add("dve", lambda e, pya=pya, ta=ta, n=n: e.tensor_tensor(
                        out=tmp[ta][:, 0:n], in0=ps[:, pya, 0:n], in1=tmp[ta][:, 0:n], op=ALU.mult),
                        reads=[f"ps{pya}", f"tmp{ta}"], writes=[f"tmp{ta}"])
                    S.add("dve", lambda e, pyb=pyb, tb=tb, n=n: e.tensor_tensor(
                        out=tmp[tb][:, 0:n], in0=ps[:, pyb, 0:n], in1=tmp[tb][:, 0:n], op=ALU.mult),
                        reads=[f"ps{pyb}", f"tmp{tb}"], writes=[f"tmp{tb}"])
                    S.add("dve", lambda e, ta=ta, tb=tb, j=j, off=off, n=n: e.tensor_tensor(
                        out=cbufb[:, j, 2 * off + n:2 * off + 2 * n], in0=tmp[ta][:, 0:n], in1=tmp[tb][:, 0:n], op=ALU.add),
                        reads=[f"tmp{ta}", f"tmp{tb}"], writes=[f"m{j}_{si}"])
                    for b in bks:
                        bfree(b)

            def sb_of(off):
                for si, (o, n, kind) in enumerate(sbs):
                    if o <= off < o + n:
                        return si, o, n
                raise AssertionError

            s6 = {}

            def s6_xload(ti):
                row0, off = tiles[ti]
                sl = ti % 2
                S.add("sp", lambda e, sl=sl, row0=row0: e.dma_start(out=xs[sl][:], in_=x_d[row0:row0 + 128, :]),
                      writes=[f"xs{sl}"], dma=f"xs{sl}")

            def s6_pload(ti):
                row0, off = tiles[ti]
                sl = ti % 2
                S.add("pool", lambda e, sl=sl, row0=row0: e.dma_start(out=pbt[sl][:], in_=p_d[row0:row0 + 128, :]),
                      writes=[f"pbt{sl}", swq()], dma=f"pbt{sl}")

            def s6_A(ti):
                row0, off = tiles[ti]
                si, o, n = sb_of(off)
                sl = ti % 2
                if ti >= 2:
                    s6_pload(ti)
                mcol = 2 * o + n + (off - o)
                bm = [balloc(), balloc()]
                for h in range(2):
                    for k in range(NCH):
                        xw = [f"ps{bm[1]}"] if (h == 0 and k == 0) else []
                        S.add("pe", lambda e, h=h, k=k, bm=bm, mcol=mcol: e.matmul(
                            ps[:, bm[h], :], lhsT=cbufb[:, k, mcol:mcol + 128], rhs=wo_t[:, k, 512 * h:512 * h + 512],
                            start=(k == 0), stop=(k == NCH - 1)),
                            reads=[f"m{k}_{si}", f"wo{k // 4}"], writes=[f"ps{bm[h]}"] + xw)
                for h in range(2):
                    S.add("dve", lambda e, h=h, sl=sl, bm=bm: e.tensor_tensor(
                        out=Abuf[sl][:, 512 * h:512 * h + 512], in0=ps[:, bm[h], :], in1=xs[sl][:, 512 * h:512 * h + 512],
                        op=ALU.add), reads=[f"ps{bm[h]}", f"xs{sl}"], writes=[f"A{sl}_{h}"])
                bfree(bm[0])
                bfree(bm[1])
                S.add("act", lambda e, sl=sl: e.activation(out=xb[sl][:], in_=Abuf[sl][:], func=AF.Copy),
                      reads=[f"A{sl}_0", f"A{sl}_1"], writes=[f"xb{sl}"])
                s6[ti] = rms_stats(Abuf[sl][:], [f"A{sl}_0", f"A{sl}_1"], xs[sl][:], [f"xs{sl}"])
                if ti + 2 < len(tiles):
                    s6_xload(ti + 2)

            def s6_B(ti):
                sl = ti % 2
                bu = balloc()
                for k in range(NCH):
                    S.add("pe", lambda e, sl=sl, k=k, bu=bu: e.transpose(out=psb[:, bu, k * 128:(k + 1) * 128],
                                                                          in_=xb[sl][:, k * 128:(k + 1) * 128], identity=idb[:]),
                          reads=[f"xb{sl}", "idb"], writes=[f"ps{bu}"])
                S.add("dve", lambda e, bu=bu: e.tensor_tensor(
                    out=u2T[:], in0=psb[:, bu, :].rearrange("p (k t) -> p k t", t=128),
                    in1=vecs[:, V_GPLE:V_GPLE + 8].unsqueeze(2).to_broadcast([128, NCH, 128]), op=ALU.mult),
                    reads=[f"ps{bu}", "vecs"], writes=["u2T"])
                bfree(bu)
                bp = balloc()
                for q in range(2):
                    S.add("pe", lambda e, sl=sl, q=q, bp=bp: e.transpose(out=psb[:, bp, q * 128:(q + 1) * 128],
                                                                          in_=pbt[sl][:, q * 128:(q + 1) * 128], identity=idb[:]),
                          reads=[f"pbt{sl}", "idb"], writes=[f"ps{bp}"])
                S.add("act", lambda e, bp=bp: e.activation(out=pT[:], in_=psb[:, bp, 0:256].rearrange("p (k t) -> p k t", t=128),
                                                           func=AF.Copy),
                      reads=[f"ps{bp}"], writes=["pT"])
                bfree(bp)

            def s6_C(ti):
                row0, off = tiles[ti]
                sl = ti % 2
                bg = [balloc(), balloc()]
                be = [balloc(), balloc()]
                for h in range(2):
                    for k in range(NCH):
                        xw = [f"ps{q}" for q in bg + be] if (h == 0 and k == 0) else []
                        xr = ["pT", "wpe"] if (h == 0 and k == 0) else []
                        S.add("pe", lambda e, h=h, k=k, bg=bg: e.matmul(
                            ps[:, bg[h], :], lhsT=u2T[:, k, :], rhs=wpg_t[:, k, 512 * h:512 * h + 512],
                            start=(k == 0), stop=(k == NCH - 1)),
                            reads=["u2T", f"wpg{k // 4}"] + xr, writes=[f"ps{bg[h]}"] + xw)
                for h in range(2):
                    for q in range(2):
                        S.add("pe", lambda e, h=h, q=q, be=be: e.matmul(
                            ps[:, be[h], :], lhsT=pT[:, q, :], rhs=wpe_t[:, q, 512 * h:512 * h + 512],
                            start=(q == 0), stop=(q == 1)),
                            reads=["pT", "wpe"], writes=[f"ps{be[h]}"])
                rs2_ap, rs2_res = s6[ti]
                bs = ti % 2
                Bt = BB[bs]
                for h in range(2):
                    S.add("act", lambda e, h=h, bg=bg, rs2_ap=rs2_ap, Bt=Bt: e.activation(
                        out=Bt[:, 512 * h:512 * h + 512], in_=ps[:, bg[h], :], func=AF.Sigmoid, scale=rs2_ap),
                        reads=[f"ps{bg[h]}", rs2_res], writes=[f"BB{bs}_{h}"])
                    S.add("dve", lambda e, h=h, be=be, Bt=Bt: e.tensor_tensor(
                        out=Bt[:, 512 * h:512 * h + 512], in0=ps[:, be[h], :], in1=Bt[:, 512 * h:512 * h + 512], op=ALU.mult),
                        reads=[f"ps{be[h]}", f"BB{bs}_{h}"], writes=[f"BB{bs}_{h}"])
                for b in bg + be:
                    bfree(b)
                S.add("dve", lambda e, sl=sl, Bt=Bt: e.tensor_tensor(out=Bt, in0=Bt, in1=Abuf[sl][:], op=ALU.add),
                      reads=[f"BB{bs}_0", f"BB{bs}_1", f"A{sl}_0", f"A{sl}_1"], writes=[f"BB{bs}_0", f"BB{bs}_1"])

            def s6_C2(ti):
                pump(2)
                row0, off = tiles[ti]
                bs = ti % 2
                Bt = BB[bs]
                rs_ap, rs_res = rms_stats(Bt, [f"BB{bs}_0", f"BB{bs}_1"], yt[:], ["yt"])
                S.add("dve", lambda e, rs_ap=rs_ap, Bt=Bt: e.scalar_tensor_tensor(
                    out=yt[:], in0=Bt, scalar=rs_ap, in1=gfin[:], op0=ALU.mult, op1=ALU.mult),
                    reads=[f"BB{bs}_0", f"BB{bs}_1", rs_res, "gfin"], writes=["yt"])
                S.add("sp", lambda e, row0=row0: e.dma_start(out=y_d[row0:row0 + 128, :], in_=yt[:]),
                      reads=["yt"], dma="yt")

            nt = len(tiles)
            s6_A(0)
            if nt > 1:
                s6_A(1)
            s6_B(0)
            for ti in range(nt):
                s6_C(ti)
                if ti + 1 < nt:
                    s6_B(ti + 1)
                if ti + 2 < nt:
                    s6_A(ti + 2)
                    if ti + 2 == nt - 1 and next_tiles:
                        s1_load(next_tiles, 1)
                        s1_load(next_tiles, 2)
                s6_C2(ti)

        out_T(lambda j: vkeep[:, j, :], 32, [f"vkeep{j}" for j in range(NCH)], Abuf[0], ["A0_0", "A0_1"],
              [(ncbp_d, 2, 32, "o_ncbp")])
        out_T(lambda j: skeep[:, j, :], 2, [f"skeep{j}" for j in range(NCH)], Abuf[1], ["A1_0", "A1_1"],
              [(ncap_d, 0, 2, "o_ncap")])
        out_T(lambda j: skeep_s[:, j, :], 32, [f"skeep_s{j}" for j in range(NCH)], xs[0], ["xs0", "xs0"],
              [(ncas_d, 0, 32, "o_ncas")])

        S.emit()
    return nc


_CACHE = {}


def kernel(x_prompt, x_sample, state_conv_a, state_conv_b, p_prompt, p_sample,
           g_mix, w_in, w_conv_a, w_out_a, w_conv_b, b_conv_b, ln_g, ln_b,
           w_out_b, w_o, w_pe, g_ple, w_pg, g_final):
    f = lambda a: np.ascontiguousarray(np.asarray(a, dtype=np.float32))
    x_prompt, x_sample = f(x_prompt), f(x_sample)
    state_conv_a, state_conv_b = f(state_conv_a), f(state_conv_b)
    p_prompt, p_sample = f(p_prompt), f(p_sample)
    NC = 8
    vecs = np.zeros((128, NV), np.float32)

    def fm(v):
        return f(v).reshape(8, 128).T

    vecs[:, V_GMIX:V_GMIX + 8] = fm(g_mix[0])
    vecs[:, V_GPLE:V_GPLE + 8] = fm(g_ple[0])
    vecs[:, V_BCONV:V_BCONV + 8] = fm(b_conv_b[0])
    vecs[:, V_LNG:V_LNG + 8] = fm(ln_g[0])
    vecs[:, V_LNB:V_LNB + 8] = fm(ln_b[0])
    wca = f(w_conv_a[0])
    vecs[:, V_WCA:V_WCA + 24] = wca.reshape(3, 8, 128).transpose(2, 1, 0).reshape(128, 24)
    wcb = f(w_conv_b[0])
    wpad = np.concatenate([wcb, np.zeros((1, D), np.float32)], axis=0)
    wl = wpad.reshape(16, 2, 8, 2, 64).transpose(1, 4, 2, 3, 0)
    vecs[:, V_WCB:V_WCB + 256] = wl.reshape(128, 256)
    vecs[:, V_MASK:V_MASK + 64] = np.tile(np.eye(64, dtype=np.float32), (2, 1))
    gfin = np.ascontiguousarray(np.broadcast_to(f(g_final)[None, :], (128, D)))
    ident = np.eye(128, dtype=np.float32)
    shared = dict(w_in=f(w_in[0]), w_oa=f(w_out_a[0]), w_ob=f(w_out_b[0]), w_o=f(w_o[0]), w_pg=f(w_pg[0]),
                  w_pe=f(w_pe[0]), vecs=vecs, gfin=gfin, ident=ident)
    in_maps = []
    for c in range(NC):
        s0 = 16 * c
        xs_ = x_sample[s0:s0 + 16].transpose(1, 0, 2).reshape(128, D)
        ps_ = p_sample[0, s0:s0 + 16].transpose(1, 0, 2).reshape(128, 256)
        m = dict(shared)
        m["x"] = np.ascontiguousarray(np.concatenate([x_prompt[c], xs_], axis=0))
        m["p"] = np.ascontiguousarray(np.concatenate([p_prompt[0, c], ps_], axis=0))
        m["sa"] = np.ascontiguousarray(state_conv_a[0, s0:s0 + 16].transpose(1, 0, 2).reshape(32, D))
        m["sb"] = np.ascontiguousarray(state_conv_b[0, s0:s0 + 16].transpose(1, 0, 2).reshape(480, D))
        in_maps.append(m)
    if "nc" not in _CACHE:
        _CACHE["nc"] = build_program()
    res = run_bass_kernel_spmd(_CACHE["nc"], in_maps, core_ids=list(range(NC)))
    R = res.results
    y_prompt = np.stack([R[c]["y"][:2048] for c in range(NC)], axis=0)
    y_sample = np.concatenate(
        [R[c]["y"][2048:].reshape(8, 16, D).transpose(1, 0, 2) for c in range(NC)], axis=0)
    nca_p = np.stack([R[c]["nca_p"] for c in range(NC)], axis=0)[None]
    ncb_p = np.stack([R[c]["ncb_p"] for c in range(NC)], axis=0)[None]
    nca_s = np.concatenate([R[c]["nca_s"].reshape(2, 16, D).transpose(1, 0, 2) for c in range(NC)], axis=0)[None]
    ncb_s = np.concatenate(
        [np.concatenate([R[c]["ncb_s_hist"].reshape(22, 16, D).transpose(1, 0, 2),
                         R[c]["ncb_s_new"].reshape(8, 16, D).transpose(1, 0, 2)], axis=1) for c in range(NC)],
        axis=0)[None]
    out = (y_prompt, y_sample, nca_p, ncb_p, nca_s, ncb_s)
    return tuple(np.ascontiguousarray(o, dtype=np.float32) for o in out)
```

```python
import contextlib
import numpy as np
import concourse.bass as bass
import concourse.mybir as mybir
from concourse.bass_utils import run_bass_kernel_spmd

F32 = mybir.dt.float32
BF16 = mybir.dt.bfloat16
AF = mybir.ActivationFunctionType
ALU = mybir.AluOpType

D = 1024
NCH = 8
KB = 31
EPS = 1e-6
LN_EPS = 1e-5
NTOK = 2176
NTB = 768

V_GMIX, V_GPLE, V_BCONV, V_LNG, V_LNB = 0, 8, 16, 24, 32
V_WCA = 40
V_WCB = 64
V_MASK = V_WCB + 256
NV = V_MASK + 64

BLOCKS = [
    dict(p0=0, npr=768, sbs=[(0, 512, "p"), (512, 256, "p")]),
    dict(p0=768, npr=768, sbs=[(0, 512, "p"), (512, 256, "p")]),
    dict(p0=1536, npr=512, sbs=[(0, 512, "p"), (512, 128, "s")]),
]


class Sched:
    def __init__(self, nc):
        self.nc = nc
        self.ops = []
        self.last_writer = {}
        self.readers = {}
        self.chan_count = {}

    def add(self, eng, fn, reads=(), writes=(), dma=None):
        idx = len(self.ops)
        chan = eng if dma is None else ("dma", dma)
        deps = set()
        raw = set()
        for r in reads:
            w = self.last_writer.get(r)
            if w is not None:
                deps.add(w)
                raw.add(w)
        for r in writes:
            w = self.last_writer.get(r)
            if w is not None:
                deps.add(w)
            for rd in self.readers.get(r, {}).values():
                deps.add(rd)
        keep = set()
        for d in deps:
            o = self.ops[d]
            if o["chan"] == eng and dma is None:
                if eng == "pe":
                    continue
            keep.add(d)
        for r in reads:
            self.readers.setdefault(r, {})[chan] = idx
        for r in writes:
            self.last_writer[r] = idx
            self.readers[r] = {}
        seq = None
        if dma is not None:
            seq = self.chan_count.get(chan, 0) + 1
            self.chan_count[chan] = seq
        self.ops.append(dict(eng=eng, chan=chan, fn=fn, deps=keep, dma=dma, seq=seq,
                             signal=dma is not None))
        for d in keep:
            self.ops[d]["signal"] = True
        return idx

    def emit(self, final_wait_eng="sp"):
        nc = self.nc
        ops = self.ops
        cnt = {}
        for o in ops:
            if o["dma"] is None and o["signal"]:
                cnt[o["chan"]] = cnt.get(o["chan"], 0) + 1
                o["seq"] = cnt[o["chan"]]
        chans = []
        for o in ops:
            if o["signal"] and o["chan"] not in chans:
                chans.append(o["chan"])
        with contextlib.ExitStack() as st:
            sems = {}
            for c in chans:
                nm = "s_" + (c if isinstance(c, str) else "d_" + str(c[1]))
                sems[c] = st.enter_context(nc.semaphore(nm))
            block = st.enter_context(nc.Block())
            final = {}
            for o in ops:
                if o["dma"] is not None:
                    final[o["chan"]] = o["seq"] * 16

            def run(engname, e):
                waited = {}
                for o in ops:
                    if o["eng"] != engname:
                        continue
                    need = {}
                    for d in o["deps"]:
                        od = ops[d]
                        v = od["seq"] * (16 if od["dma"] is not None else 1)
                        if v > need.get(od["chan"], 0):
                            need[od["chan"]] = v
                    for c, v in need.items():
                        if waited.get(c, 0) >= v:
                            continue
                        e.wait_ge(sems[c], v)
                        waited[c] = v
                    ins = o["fn"](e)
                    if o["signal"]:
                        ins.then_inc(sems[o["chan"]], 16 if o["dma"] is not None else 1)
                if engname == final_wait_eng:
                    for c, v in final.items():
                        if waited.get(c, 0) < v:
                            e.wait_ge(sems[c], v)

            @block.tensor
            def _(e):
                run("pe", e)

            @block.scalar
            def _(e):
                run("act", e)

            @block.vector
            def _(e):
                run("dve", e)

            @block.gpsimd
            def _(e):
                run("pool", e)

            @block.sync
            def _(e):
                run("sp", e)


def build_program():
    nc = bass.Bass("TRN2", target_bir_lowering=False)

    def din(name, shape):
        return nc.dram_tensor(name, shape, F32, kind="ExternalInput").ap()

    def dout(name, shape):
        return nc.dram_tensor(name, shape, F32, kind="ExternalOutput").ap()

    x_d = din("x", [NTOK, D])
    p_d = din("p", [NTOK, 256])
    sa_d = din("sa", [32, D])
    sb_d = din("sb", [480, D])
    w_in = din("w_in", [D, 9 * D])
    w_oa = din("w_oa", [D, D])
    w_ob = din("w_ob", [D, D])
    w_o = din("w_o", [D, D])
    w_pg = din("w_pg", [D, D])
    w_pe = din("w_pe", [256, D])
    vecs_d = din("vecs", [128, NV])
    gfin_d = din("gfin", [128, D])
    ident_d = din("ident", [128, 128])

    y_d = dout("y", [NTOK, D])
    ncap_d = dout("nca_p", [2, D])
    ncbp_d = dout("ncb_p", [30, D])
    ncas_d = dout("nca_s", [32, D])
    ncbs_new_d = dout("ncb_s_new", [128, D])
    ncbs_hist_d = dout("ncb_s_hist", [352, D])

    with contextlib.ExitStack() as st:
        def sb(name, shape, dt):
            return st.enter_context(nc.sbuf_tensor("t_" + name, shape, dt))

        idf = sb("idf", [128, 128], F32)
        idb = sb("idb", [128, 128], BF16)
        onesf = sb("onesf", [128, 128], F32)
        vecs = sb("vecs", [128, NV], F32)
        gfin = sb("gfin", [128, D], F32)
        mhalf = sb("mhalf", [128, 1], F32)
        histb = sb("histb", [128, NCH, 480], BF16)
        hista = sb("hista", [128, NCH, 32], F32)
        halo_v = sb("halo_v", [128, NCH, 30], BF16)
        halo_s = sb("halo_s", [128, NCH, 2], F32)
        uT = sb("uT", [128, NCH, NTB], BF16)
        yaT = sb("yaT", [128, NCH, NTB], BF16)
        cbuf = sb("cbuf", [128, NCH, NTB], F32)
        cbufb = cbuf.bitcast(BF16)
        NRING = 8
        ring = [sb(f"ring{i}", [128, NCH, 128], BF16) for i in range(NRING)]
        wo_t = sb("wo_t", [128, NCH, D], BF16)
        wpg_t = sb("wpg_t", [128, NCH, D], BF16)
        wpe_t = sb("wpe_t", [128, 2, D], BF16)
        Wp = [sb(f"Wp{i}", [128, 32, 64], BF16) for i in range(2)]
        RL = 608
        NRS = 3
        Rb = [[sb(f"R{i}_{g}", [128, RL], BF16) for g in range(2)] for i in range(NRS)]
        vext = [sb(f"vext{i}", [128, 30 + NTB + 2], BF16) for i in range(2)]
        vexts = [sb(f"vexts{i}", [128, 624], BF16) for i in range(2)]
        sext = [sb(f"sext{i}", [128, 2 + NTB], F32) for i in range(1)]
        sexts = [sb(f"sexts{i}", [128, 160], F32) for i in range(1)]
        xs = [sb(f"xs{i}", [128, D], F32) for i in range(3)]
        xb = [sb(f"xb{i}", [128, D], BF16) for i in range(2)]
        ssq = sb("ssq", [128, 64], F32)
        msq = sb("msq", [128, 64], F32)
        rsd = sb("rsd", [128, 64], F32)
        NTMP = 6
        tmp = [sb(f"tmp{i}", [128, 512], F32) for i in range(NTMP)]
        accs = sb("accs", [128, NTB], F32)
        accq = sb("accq", [128, NTB], F32)
        rstdB = sb("rstdB", [128, NTB], F32)
        nmrB = sb("nmrB", [128, NTB], F32)
        Abuf = [sb(f"Abuf{i}", [128, D], F32) for i in range(2)]
        Bbuf = sb("Bbuf", [128, D], F32)
        yt = sb("yt", [128, D], F32)
        u2T = sb("u2T", [128, NCH, 128], BF16)
        pbt = [sb(f"pbt{i}", [128, 256], BF16) for i in range(2)]
        pT = sb("pT", [128, 2, 128], BF16)
        vkeep = sb("vkeep", [128, NCH, 32], F32)
        skeep = sb("skeep", [128, NCH, 2], F32)
        vkeep_s = sb("vkeep_s", [128, NCH, 128], F32)
        BB = [Bbuf[:, :], vkeep_s.rearrange("p a b -> p (a b)")]
        skeep_s = sb("skeep_s", [128, NCH, 32], F32)
        ps = st.enter_context(nc.psum_tensor("ps", [128, 8, 512], F32))
        psb = ps.bitcast(BF16)

        S = Sched(nc)

        free_banks = list(range(8))

        def balloc():
            return free_banks.pop(0)

        def bfree(b):
            free_banks.append(b)

        tmp_ctr = [0]

        def talloc():
            k = tmp_ctr[0] % NTMP
            tmp_ctr[0] += 1
            return k

        ring_ctr = [0]

        SWQ = 4
        swq_ctr = [0]

        def swq():
            r = f"swq{swq_ctr[0] % SWQ}"
            swq_ctr[0] += 1
            return r

        WL, WG = [], []

        def add_group(srcs):
            WG.append(list(range(len(WL), len(WL) + len(srcs))))
            for sap in srcs:
                WL.append((sap, len(WG) - 1))

        def wcol(c0, j):
            return w_in[:, c0 * D + 128 * j:c0 * D + 128 * j + 128]

        for _bi in range(len(BLOCKS)):
            for j in range(NCH):
                add_group([wcol(4, j), wcol(5, j)])
            for j in range(NCH):
                add_group([wcol(1, j), wcol(2, j), wcol(0, j), wcol(3, j)])
                if j >= 4:
                    add_group([wcol(6, 2 * (j - 4))])
                    add_group([wcol(6, 2 * (j - 4) + 1)])
            for j in range(NCH):
                add_group([w_oa[:, 128 * j:128 * j + 128], w_ob[:, 128 * j:128 * j + 128], wcol(7, j), wcol(8, j)])
        wst = dict(nl=0, ng=0, done=set())

        def pump(maxn=4):
            nem = 0
            while nem < maxn and wst["nl"] < len(WL) and (wst["nl"] < NRING or WL[wst["nl"] - NRING][1] in wst["done"]):
                nem += 1
                t = wst["nl"]
                wst["nl"] += 1
                k = t % NRING
                src = WL[t][0]
                S.add("pool", lambda e, k=k, src=src: e.dma_start(
                    out=ring[k][:, :, :], in_=src.rearrange("(k p) m -> p k m", p=128)),
                    writes=[f"ring{k}", swq()], dma=f"ring{k}")

        def wgroup():
            g = wst["ng"]
            wst["ng"] += 1
            if not all(t < wst["nl"] for t in WG[g]):
                pump(NRING)
            assert all(t < wst["nl"] for t in WG[g]), "weight tile not prefetched"
            return g, [t % NRING for t in WG[g]]

        def wdone(g):
            wst["done"].add(g)
            pump()

        stat_ctr = [0]

        def rms_stats(src_ap, src_res, junk_ap, junk_res):
            c = stat_ctr[0] % 64
            stat_ctr[0] += 1
            S.add("act", lambda e, c=c: e.activation(out=junk_ap, in_=src_ap, func=AF.Square,
                                                     accum_out=ssq[:, c:c + 1]),
                  reads=src_res, writes=list(junk_res) + [f"ssq{c}"])
            S.add("pool", lambda e, c=c: e.tensor_scalar(out=msq[:, c:c + 1], in0=ssq[:, c:c + 1],
                                                         scalar1=1.0 / D, scalar2=EPS,
                                                         op0=ALU.mult, op1=ALU.add),
                  reads=[f"ssq{c}"], writes=[f"msq{c}"])
            S.add("pool", lambda e, c=c: e.tensor_tensor(out=rsd[:, c:c + 1], in0=msq[:, c:c + 1],
                                                         in1=mhalf[:, 0:1], op=ALU.pow),
                  reads=[f"msq{c}", "mhalf"], writes=[f"rsd{c}"])
            return rsd[:, c:c + 1], f"rsd{c}"

        def vcol(off, j):
            return vecs[:, off + j:off + j + 1]

        S.add("sp", lambda e: e.dma_start(out=idf[:], in_=ident_d), writes=["idf"], dma="idf")
        S.add("sp", lambda e: e.dma_start(out=vecs[:], in_=vecs_d), writes=["vecs"], dma="vecs")
        S.add("sp", lambda e: e.dma_start(out=gfin[:], in_=gfin_d), writes=["gfin"], dma="gfin")
        S.add("dve", lambda e: e.tensor_copy(out=idb[:], in_=idf[:]), reads=["idf"], writes=["idb"])
        S.add("pool", lambda e: e.memset(onesf[:], 1.0), writes=["onesf"])
        for i in range(2):
            S.add("pool", lambda e, i=i: e.memset(vext[i][:], 0.0), writes=[f"vext{i}h", f"vext{i}_0", f"vext{i}_1"])
            S.add("pool", lambda e, i=i: e.memset(vexts[i][:], 0.0), writes=[f"vexts{i}h", f"vexts{i}n"])
        S.add("pool", lambda e: e.memset(mhalf[:], -0.5), writes=["mhalf"])
        S.add("pool", lambda e: e.memset(halo_v[:], 0.0), writes=[f"halo_v{j}" for j in range(NCH)])
        S.add("pool", lambda e: e.memset(halo_s[:], 0.0), writes=[f"halo_s{j}" for j in range(NCH)])
        pump()
        RESIDENT = []
        for h in range(2):
            RESIDENT.append(lambda h=h: S.add("pool", lambda e: e.dma_start(
                out=wo_t[:, 4 * h:4 * h + 4, :],
                in_=w_o[512 * h:512 * h + 512, :].rearrange("(k p) m -> p k m", p=128)),
                writes=[f"wo{h}", swq()], dma=f"wo{h}"))
        for h in range(2):
            RESIDENT.append(lambda h=h: S.add("pool", lambda e: e.dma_start(
                out=wpg_t[:, 4 * h:4 * h + 4, :],
                in_=w_pg[512 * h:512 * h + 512, :].rearrange("(k p) m -> p k m", p=128)),
                writes=[f"wpg{h}", swq()], dma=f"wpg{h}"))
        RESIDENT.append(lambda: S.add("pool", lambda e: e.dma_start(
            out=wpe_t[:], in_=w_pe.rearrange("(k p) m -> p k m", p=128)),
            writes=["wpe", swq()], dma="wpe"))

        def hist_setup():
            rows_list = [(0, 128), (128, 128), (256, 128), (384, 96)]
            for ti, (r0, nr) in enumerate(rows_list):
                sl = ti % 2
                S.add("sp", lambda e, sl=sl, r0=r0, nr=nr: e.dma_start(out=Abuf[sl][0:nr, :], in_=sb_d[r0:r0 + nr, :]),
                      writes=[f"A{sl}_0", f"A{sl}_1"], dma=f"hst{sl}")
                b0, b1 = balloc(), balloc()
                for k in range(NCH):
                    b = b0 if k < 4 else b1
                    S.add("pe", lambda e, sl=sl, k=k, b=b, nr=nr: e.transpose(
                        out=ps[:, b, (k % 4) * 128:(k % 4) * 128 + nr], in_=Abuf[sl][0:nr, k * 128:(k + 1) * 128],
                        identity=idf[0:nr, 0:nr]),
                        reads=[f"A{sl}_0", f"A{sl}_1", "idf"], writes=[f"ps{b}"])
                for h, b in enumerate((b0, b1)):
                    S.add("act", lambda e, h=h, b=b, r0=r0, nr=nr: e.activation(
                        out=histb[:, 4 * h:4 * h + 4, r0:r0 + nr],
                        in_=ps[:, b, :].rearrange("p (k t) -> p k t", t=128)[:, :, 0:nr], func=AF.Copy),
                        reads=[f"ps{b}"], writes=[f"histb{ti}_{h}"])
                bfree(b0)
                bfree(b1)
            S.add("sp", lambda e: e.dma_start(out=Abuf[0][0:32, :], in_=sa_d), writes=["A0_0", "A0_1"], dma="hst0")
            b0 = balloc()
            for k in range(NCH):
                S.add("pe", lambda e, k=k, b0=b0: e.transpose(out=ps[:, b0, k * 32:(k + 1) * 32],
                                                              in_=Abuf[0][0:32, k * 128:(k + 1) * 128],
                                                              identity=idf[0:32, 0:32]),
                      reads=["A0_0", "A0_1", "idf"], writes=[f"ps{b0}"])
            S.add("act", lambda e, b0=b0: e.activation(out=hista[:], in_=ps[:, b0, 0:256].rearrange("p (k t) -> p k t", t=32),
                                                       func=AF.Copy),
                  reads=[f"ps{b0}"], writes=["hista"])
            bfree(b0)
            S.add("sp", lambda e: e.dma_start(out=ncbs_hist_d, in_=sb_d[128:480, :]), dma="ncbs_hist")

        HISTB_RES = [f"histb{ti}_{h}" for ti in range(4) for h in range(2)]

        def tiles_of(blk):
            t = [(blk["p0"] + 128 * i, 128 * i) for i in range(blk["npr"] // 128)]
            if any(k == "s" for _, _, k in blk["sbs"]):
                t.append((2048, blk["npr"]))
            return t

        s1st = {}

        s1_loaded = set()

        def s1_load(tl, ti):
            s1_loaded.add((tl[0][0], ti))
            row0, off = tl[ti]
            xsl = (ti + 2) % 3
            S.add("sp", lambda e, xsl=xsl, row0=row0: e.dma_start(out=xs[xsl][:], in_=x_d[row0:row0 + 128, :]),
                  writes=[f"xs{xsl}"], dma=f"xs{xsl}")

        s1_sdone, s1_rdone = set(), set()

        def s1_stats(tl, ti):
            s1_sdone.add((tl[0][0], ti))
            xsl = (ti + 2) % 3
            sl = ti % 2
            s1st[ti] = rms_stats(xs[xsl][:], [f"xs{xsl}"], xb[sl][:], [f"xb{sl}"])

        def s1_rest(tl, ti):
            s1_rdone.add((tl[0][0], ti))
            pump(2)
            row0, off = tl[ti]
            xsl, sl = (ti + 2) % 3, ti % 2
            rs_ap, rs_res = s1st[ti]
            S.add("act", lambda e, sl=sl, xsl=xsl, rs_ap=rs_ap: e.activation(out=xb[sl][:], in_=xs[xsl][:], func=AF.Copy,
                                                                             scale=rs_ap),
                  reads=[f"xs{xsl}", rs_res], writes=[f"xb{sl}"])
            b = balloc()
            for k in range(NCH):
                S.add("pe", lambda e, sl=sl, k=k, b=b: e.transpose(out=psb[:, b, k * 128:(k + 1) * 128],
                                                                    in_=xb[sl][:, k * 128:(k + 1) * 128],
                                                                    identity=idb[:]),
                      reads=[f"xb{sl}", "idb"], writes=[f"ps{b}"])
            S.add("dve", lambda e, b=b, off=off: e.tensor_tensor(
                out=uT[:, :, off:off + 128], in0=psb[:, b, :].rearrange("p (k t) -> p k t", t=128),
                in1=vecs[:, V_GMIX:V_GMIX + 8].unsqueeze(2).to_broadcast([128, NCH, 128]), op=ALU.mult),
                reads=[f"ps{b}", "vecs"], writes=[f"uT{off // 128}"])
            bfree(b)

        def out_T(src_fn, nrow, res_list, stage, stage_res, dsts, extra_reads=()):
            b0, b1 = balloc(), balloc()
            for j in range(NCH):
                b = b0 if j < 4 else b1
                S.add("pe", lambda e, j=j, b=b: e.transpose(out=ps[0:nrow, b, (j % 4) * 128:(j % 4) * 128 + 128],
                                                             in_=src_fn(j), identity=idf[:]),
                      reads=[res_list[j], "idf"] + list(extra_reads), writes=[f"ps{b}"])
            for h, b in enumerate((b0, b1)):
                S.add("act", lambda e, h=h, b=b: e.activation(out=stage[0:nrow, 512 * h:512 * h + 512], in_=ps[0:nrow, b, :],
                                                              func=AF.Copy),
                      reads=[f"ps{b}"], writes=[stage_res[h]])
            bfree(b0)
            bfree(b1)
            for (dst, r0, r1, key) in dsts:
                S.add("sp", lambda e, dst=dst, r0=r0, r1=r1: e.dma_start(out=dst, in_=stage[r0:r1, :]),
                      reads=stage_res, dma=key)

        for bi, blk in enumerate(BLOCKS):
            p0, npr, sbs = blk["p0"], blk["npr"], blk["sbs"]
            last = bi == len(BLOCKS) - 1
            has_s = any(k == "s" for _, _, k in sbs)
            tiles = tiles_of(blk)
            next_tiles = tiles_of(BLOCKS[bi + 1]) if not last else []

            def ures(off, n):
                return [f"uT{t}" for t in range(off // 128, (off + n) // 128)]

            for ti in range(min(3, len(tiles))):
                if (tiles[0][0], ti) not in s1_loaded:
                    s1_load(tiles, ti)
            p0k = tiles[0][0]
            for ti in range(len(tiles) + 1):
                if ti < len(tiles) and (p0k, ti) not in s1_sdone:
                    s1_stats(tiles, ti)
                if ti >= 1:
                    if (p0k, ti - 1) not in s1_rdone:
                        s1_rest(tiles, ti - 1)
                    if ti + 2 < len(tiles) and (p0k, ti + 2) not in s1_loaded:
                        s1_load(tiles, ti + 2)

            jobs = [(j, si) for j in range(NCH) for si in range(len(sbs))]
            s2state = {}
            s2rs = {}
            s2_jobctr = [0]

            def diag_build(j):
                ds = j % 2
                S.add("dve", lambda e, ds=ds, j=j: e.tensor_tensor(
                    out=Wp[ds][:], in0=vecs[:, V_MASK:V_MASK + 64].unsqueeze(1).to_broadcast([128, 32, 64]),
                    in1=vecs[:, V_WCB + 32 * j:V_WCB + 32 * j + 32].unsqueeze(2).to_broadcast([128, 32, 64]),
                    op=ALU.mult), reads=["vecs"], writes=[f"diag{ds}"])

            def s2_proj(j, si):
                off, n, kind = sbs[si]
                if si == 0:
                    wg_, (kv, kg) = wgroup()
                    if bi == 0 and j < len(RESIDENT):
                        RESIDENT[j]()
                    ds = j % 2
                    if j < 2:
                        diag_build(j)
                    vs = j % 2
                    S.add("pool", lambda e, vs=vs, j=j: e.tensor_copy(out=vext[vs][:, 0:30], in_=halo_v[:, j, :]),
                          reads=[f"halo_v{j}"], writes=[f"vext{vs}h"])
                    if has_s:
                        S.add("pool", lambda e, vs=vs, j=j: e.tensor_copy(out=vexts[vs][:, 0:480], in_=histb[:, j, :]),
                              reads=HISTB_RES, writes=[f"vexts{vs}h"])
                    s2state[j] = (kv, kg, ds, vs, wg_)
                kv, kg, ds, vs, wg_ = s2state[j]
                bv, bg = balloc(), balloc()
                first = True
                for (b, kw) in ((bv, kv), (bg, kg)):
                    for k in range(NCH):
                        xr = [f"ring{kv}", f"ring{kg}"] if first else []
                        xw = [f"ps{bv}", f"ps{bg}"] if first else []
                        first = False
                        S.add("pe", lambda e, b=b, kw=kw, k=k, off=off, n=n: e.matmul(
                            ps[:, b, 0:n], lhsT=ring[kw][:, k, :], rhs=uT[:, k, off:off + n],
                            start=(k == 0), stop=(k == NCH - 1)),
                            reads=[f"ring{kw}"] + ures(off, n) + xr, writes=[f"ps{b}"] + xw)
                if si == len(sbs) - 1:
                    wdone(wg_)
                tg = talloc()
                S.add("act", lambda e, bg=bg, tg=tg, n=n: e.activation(out=tmp[tg][:, 0:n], in_=ps[:, bg, 0:n],
                                                                       func=AF.Sigmoid),
                      reads=[f"ps{bg}"], writes=[f"tmp{tg}"])
                if kind == "p":
                    vdst, vres = vext[vs][:, 30 + off:30 + off + n], f"vext{vs}_{si}"
                else:
                    vdst, vres = vexts[vs][:, 480:608], f"vexts{vs}n"
                S.add("dve", lambda e, bv=bv, tg=tg, n=n, vdst=vdst: e.tensor_tensor(
                    out=vdst, in0=ps[:, bv, 0:n], in1=tmp[tg][:, 0:n], op=ALU.mult),
                    reads=[f"ps{bv}", f"tmp{tg}"], writes=[vres])
                rs = s2_jobctr[0] % NRS
                s2_jobctr[0] += 1
                s2rs[(j, si)] = rs
                if kind == "p":
                    rr = [f"vext{vs}h"] + [f"vext{vs}_{q}" for q in range(si + 1)]
                    if si + 1 < len(sbs) and sbs[si + 1][2] == "p":
                        rr.append(f"vext{vs}_{si + 1}")
                else:
                    rr = [f"vexts{vs}h", f"vexts{vs}n"]
                for g in range(2):
                    for jj in range(2):
                        if kind == "p":
                            src = vext[vs][64 * g:64 * g + 64, off + jj:off + jj + n + 30]
                            dst = Rb[rs][g][64 * jj:64 * jj + 64, 0:n + 30]
                        else:
                            src = vexts[vs][64 * g:64 * g + 64, 16 * jj:16 * jj + 608]
                            dst = Rb[rs][g][64 * jj:64 * jj + 64, 0:608]
                        S.add("sp", lambda e, src=src, dst=dst: e.dma_start(out=dst, in_=src),
                              reads=rr, writes=[f"R{rs}_{g}_{jj}"], dma=f"R{rs}_{g}_{jj}")
                if last:
                    if kind == "p":
                        S.add("dve", lambda e, bv=bv, tg=tg, j=j: e.tensor_tensor(
                            out=vkeep[:, j, :], in0=ps[:, bv, 480:512], in1=tmp[tg][:, 480:512], op=ALU.mult),
                            reads=[f"ps{bv}", f"tmp{tg}"], writes=[f"vkeep{j}"])
                    else:
                        S.add("dve", lambda e, bv=bv, tg=tg, j=j: e.tensor_tensor(
                            out=vkeep_s[:, j, :], in0=ps[:, bv, 0:128], in1=tmp[tg][:, 0:128], op=ALU.mult),
                            reads=[f"ps{bv}", f"tmp{tg}"], writes=[f"vkeep_s{j}", "BB1_0", "BB1_1"])
                bfree(bv)
                bfree(bg)

            def s2_conv(j, si):
                off, n, kind = sbs[si]
                kv, kg, ds, vs, wg_ = s2state[j]
                bc = balloc()
                if kind == "p":
                    rres = [f"vext{vs}h"] + [f"vext{vs}_{q}" for q in range(si + 1)]
                else:
                    rres = [f"vexts{vs}h", f"vexts{vs}n"]
                rs = s2rs[(j, si)]
                for i in range(16):
                    for g in range(2):
                        if kind == "p":
                            rhs = Rb[rs][g][:, 2 * i:2 * i + n]
                        else:
                            rhs = Rb[rs][g][:, 32 * i:32 * i + 128]
                        S.add("pe", lambda e, bc=bc, ds=ds, i=i, g=g, rhs=rhs, n=n: e.matmul(
                            ps[64 * g:64 * g + 64, bc, 0:n], lhsT=Wp[ds][:, 16 * g + i, :], rhs=rhs,
                            start=(i == 0), stop=(i == 15), tile_position=(0, 64 * g)),
                            reads=[f"diag{ds}"] + [f"R{rs}_{g}_{jj}" for jj in range(2)], writes=[f"ps{bc}"])
                if si == len(sbs) - 1 and j + 2 < NCH:
                    diag_build(j + 2)
                cres = f"c{j}_{si}"
                S.add("act", lambda e, bc=bc, j=j, off=off, n=n: e.activation(
                    out=cbuf[:, j, off:off + n], in_=ps[:, bc, 0:n], func=AF.Identity, bias=vcol(V_BCONV, j)),
                    reads=[f"ps{bc}", "vecs"], writes=[cres, f"yb{j}_{si}", f"m{j}_{si}"])
                tq = talloc()
                S.add("act", lambda e, bc=bc, j=j, n=n, tq=tq: e.activation(
                    out=tmp[tq][:, 0:n], in_=ps[:, bc, 0:n], func=AF.Square, bias=vcol(V_BCONV, j)),
                    reads=[f"ps{bc}", "vecs"], writes=[f"tmp{tq}"])
                bfree(bc)
                if j == 0:
                    S.add("dve", lambda e, off=off, n=n: e.tensor_copy(out=accs[:, off:off + n], in_=cbuf[:, 0, off:off + n]),
                          reads=[cres], writes=[f"accs{si}"])
                    S.add("dve", lambda e, off=off, n=n, tq=tq: e.tensor_copy(out=accq[:, off:off + n], in_=tmp[tq][:, 0:n]),
                          reads=[f"tmp{tq}"], writes=[f"accq{si}"])
                else:
                    S.add("dve", lambda e, j=j, off=off, n=n: e.tensor_tensor(
                        out=accs[:, off:off + n], in0=accs[:, off:off + n], in1=cbuf[:, j, off:off + n], op=ALU.add),
                        reads=[cres, f"accs{si}"], writes=[f"accs{si}"])
                    S.add("dve", lambda e, off=off, n=n, tq=tq: e.tensor_tensor(
                        out=accq[:, off:off + n], in0=accq[:, off:off + n], in1=tmp[tq][:, 0:n], op=ALU.add),
                        reads=[f"tmp{tq}", f"accq{si}"], writes=[f"accq{si}"])
                nps = sum(1 for _, _, kk in sbs if kk == "p")
                if kind == "p" and si == nps - 1 and not last:
                    S.add("pool", lambda e, vs=vs, j=j, npr=npr: e.tensor_copy(out=halo_v[:, j, :], in_=vext[vs][:, npr:npr + 30]),
                          reads=[f"vext{vs}_{q}" for q in range(nps)], writes=[f"halo_v{j}"])

            def ln_stats():
                for si, (off, n, kind) in enumerate(sbs):
                    b1, b2 = balloc(), balloc()
                    S.add("pe", lambda e, b1=b1, off=off, n=n: e.matmul(ps[:, b1, 0:n], lhsT=onesf[:], rhs=accs[:, off:off + n],
                                                                         start=True, stop=True),
                          reads=["onesf", f"accs{si}"], writes=[f"ps{b1}"])
                    S.add("pe", lambda e, b2=b2, off=off, n=n: e.matmul(ps[:, b2, 0:n], lhsT=onesf[:], rhs=accq[:, off:off + n],
                                                                         start=True, stop=True),
                          reads=["onesf", f"accq{si}"], writes=[f"ps{b2}"])
                    tm, tv = talloc(), talloc()
                    S.add("dve", lambda e, b1=b1, tm=tm, n=n: e.tensor_scalar(
                        out=tmp[tm][:, 0:n], in0=ps[:, b1, 0:n], scalar1=1.0 / D, scalar2=None, op0=ALU.mult),
                        reads=[f"ps{b1}"], writes=[f"tmp{tm}"])
                    S.add("dve", lambda e, tm=tm, tv=tv, n=n: e.tensor_tensor(
                        out=tmp[tv][:, 0:n], in0=tmp[tm][:, 0:n], in1=tmp[tm][:, 0:n], op=ALU.mult),
                        reads=[f"tmp{tm}"], writes=[f"tmp{tv}"])
                    S.add("dve", lambda e, b2=b2, tv=tv, n=n: e.scalar_tensor_tensor(
                        out=tmp[tv][:, 0:n], in0=ps[:, b2, 0:n], scalar=1.0 / D, in1=tmp[tv][:, 0:n],
                        op0=ALU.mult, op1=ALU.subtract),
                        reads=[f"ps{b2}", f"tmp{tv}"], writes=[f"tmp{tv}"])
                    S.add("dve", lambda e, tv=tv, n=n: e.tensor_scalar(
                        out=tmp[tv][:, 0:n], in0=tmp[tv][:, 0:n], scalar1=LN_EPS, scalar2=None, op0=ALU.add),
                        reads=[f"tmp{tv}"], writes=[f"tmp{tv}"])
                    S.add("act", lambda e, tv=tv, n=n: e.activation(out=tmp[tv][:, 0:n], in_=tmp[tv][:, 0:n], func=AF.Ln),
                          reads=[f"tmp{tv}"], writes=[f"tmp{tv}"])
                    S.add("act", lambda e, tv=tv, off=off, n=n: e.activation(
                        out=rstdB[:, off:off + n], in_=tmp[tv][:, 0:n], func=AF.Exp, scale=-0.5),
                        reads=[f"tmp{tv}"], writes=[f"rstdB{si}"])
                    S.add("dve", lambda e, tm=tm, off=off, n=n: e.scalar_tensor_tensor(
                        out=nmrB[:, off:off + n], in0=tmp[tm][:, 0:n], scalar=-1.0, in1=rstdB[:, off:off + n],
                        op0=ALU.mult, op1=ALU.mult),
                        reads=[f"tmp{tm}", f"rstdB{si}"], writes=[f"nmrB{si}"])
                    bfree(b1)
                    bfree(b2)

            def s4_chunk(j):
                wg4, (kz,) = wgroup()
                for si, (off, n, kind) in enumerate(sbs):
                    pz = balloc()
                    for k in range(NCH):
                        S.add("pe", lambda e, pz=pz, kz=kz, k=k, off=off, n=n: e.matmul(
                            ps[:, pz, 0:n], lhsT=ring[kz][:, k, :], rhs=uT[:, k, off:off + n],
                            start=(k == 0), stop=(k == NCH - 1)),
                            reads=[f"ring{kz}"] + ures(off, n), writes=[f"ps{pz}"])
                    if si == len(sbs) - 1:
                        wdone(wg4)
                    tx = talloc()
                    S.add("dve", lambda e, tx=tx, j=j, off=off, n=n: e.tensor_tensor(
                        out=tmp[tx][:, 0:n], in0=cbuf[:, j, off:off + n], in1=rstdB[:, off:off + n], op=ALU.mult),
                        reads=[f"c{j}_{si}", f"rstdB{si}"], writes=[f"tmp{tx}"])
                    S.add("dve", lambda e, tx=tx, off=off, n=n: e.tensor_tensor(
                        out=tmp[tx][:, 0:n], in0=tmp[tx][:, 0:n], in1=nmrB[:, off:off + n], op=ALU.add),
                        reads=[f"tmp{tx}", f"nmrB{si}"], writes=[f"tmp{tx}"])
                    tl = talloc()
                    S.add("act", lambda e, tx=tx, tl=tl, j=j, n=n: e.activation(
                        out=tmp[tl][:, 0:n], in_=tmp[tx][:, 0:n], func=AF.Silu, scale=vcol(V_LNG, j), bias=vcol(V_LNB, j)),
                        reads=[f"tmp{tx}", "vecs"], writes=[f"tmp{tl}"])
                    tb = talloc()
                    S.add("act", lambda e, pz=pz, tb=tb, n=n: e.activation(out=tmp[tb][:, 0:n], in_=ps[:, pz, 0:n], func=AF.Silu),
                          reads=[f"ps{pz}"], writes=[f"tmp{tb}"])
                    S.add("dve", lambda e, tl=tl, tb=tb, j=j, off=off, n=n: e.tensor_tensor(
                        out=cbufb[:, j, 2 * off:2 * off + n], in0=tmp[tl][:, 0:n], in1=tmp[tb][:, 0:n], op=ALU.mult),
                        reads=[f"tmp{tl}", f"tmp{tb}"], writes=[f"yb{j}_{si}", f"c{j}_{si}"])
                    bfree(pz)

            def s3_chunk(j):
                if j == 3:
                    ln_stats()
                wg3, (kc, kh, kb_, kz) = wgroup()
                ss_ = 0
                S.add("pool", lambda e, ss_=ss_, j=j: e.tensor_copy(out=sext[ss_][:, 0:2], in_=halo_s[:, j, :]),
                      reads=[f"halo_s{j}"], writes=[f"sext{ss_}h"])
                if has_s:
                    S.add("pool", lambda e, ss_=ss_, j=j: e.tensor_copy(out=sexts[ss_][:, 0:32], in_=hista[:, j, :]),
                          reads=["hista"], writes=[f"sexts{ss_}h"])
                nps = sum(1 for _, _, kk in sbs if kk == "p")
                for si, (off, n, kind) in enumerate(sbs):
                    bks = [balloc() for _ in range(4)]
                    first = True
                    for (b, kw) in zip(bks, (kc, kh, kb_, kz)):
                        for k in range(NCH):
                            xr = [f"ring{q}" for q in (kc, kh, kb_, kz)] if first else []
                            xw = [f"ps{q}" for q in bks] if first else []
                            first = False
                            S.add("pe", lambda e, b=b, kw=kw, k=k, off=off, n=n: e.matmul(
                                ps[:, b, 0:n], lhsT=ring[kw][:, k, :], rhs=uT[:, k, off:off + n],
                                start=(k == 0), stop=(k == NCH - 1)),
                                reads=[f"ring{kw}"] + ures(off, n) + xr, writes=[f"ps{b}"] + xw)
                    pc, ph, pb, pz = bks
                    if si == len(sbs) - 1:
                        wdone(wg3)
                    th = talloc()
                    S.add("act", lambda e, ph=ph, th=th, n=n: e.activation(out=tmp[th][:, 0:n], in_=ps[:, ph, 0:n], func=AF.Copy),
                          reads=[f"ps{ph}"], writes=[f"tmp{th}"])
                    if kind == "p":
                        sdst, sres = sext[ss_][:, 2 + off:2 + off + n], f"sext{ss_}_{si}"
                        taps = [sext[ss_][:, off + k:off + k + n] for k in range(3)]
                        rres = [f"sext{ss_}h"] + [f"sext{ss_}_{q}" for q in range(si + 1)]
                    else:
                        sdst, sres = sexts[ss_][:, 32:160], f"sexts{ss_}n"
                        taps = [sexts[ss_][:, 16 * k:16 * k + 128] for k in range(3)]
                        rres = [f"sexts{ss_}h", f"sexts{ss_}n"]
                    S.add("dve", lambda e, pc=pc, th=th, n=n, sdst=sdst: e.tensor_tensor(
                        out=sdst, in0=ps[:, pc, 0:n], in1=tmp[th][:, 0:n], op=ALU.mult),
                        reads=[f"ps{pc}", f"tmp{th}"], writes=[sres])
                    tz = talloc()
                    S.add("act", lambda e, pz=pz, tz=tz, n=n: e.activation(out=tmp[tz][:, 0:n], in_=ps[:, pz, 0:n], func=AF.Silu),
                          reads=[f"ps{pz}"], writes=[f"tmp{tz}"])
                    S.add("dve", lambda e, pb=pb, tz=tz, n=n: e.tensor_tensor(
                        out=tmp[tz][:, 0:n], in0=ps[:, pb, 0:n], in1=tmp[tz][:, 0:n], op=ALU.mult),
                        reads=[f"ps{pb}", f"tmp{tz}"], writes=[f"tmp{tz}"])
                    for b in bks:
                        bfree(b)
                    t2 = talloc()
                    S.add("dve", lambda e, t2=t2, taps=taps, n=n, j=j: e.tensor_scalar(
                        out=tmp[t2][:, 0:n], in0=taps[0], scalar1=vecs[:, V_WCA + 3 * j:V_WCA + 3 * j + 1], scalar2=None,
                        op0=ALU.mult), reads=rres + ["vecs"], writes=[f"tmp{t2}"])
                    for k in (1, 2):
                        S.add("dve", lambda e, t2=t2, taps=taps, n=n, j=j, k=k: e.scalar_tensor_tensor(
                            out=tmp[t2][:, 0:n], in0=taps[k], scalar=vecs[:, V_WCA + 3 * j + k:V_WCA + 3 * j + k + 1],
                            in1=tmp[t2][:, 0:n], op0=ALU.mult, op1=ALU.add),
                            reads=rres + ["vecs", f"tmp{t2}"], writes=[f"tmp{t2}"])
                    S.add("dve", lambda e, t2=t2, tz=tz, j=j, off=off, n=n: e.tensor_tensor(
                        out=yaT[:, j, off:off + n], in0=tmp[t2][:, 0:n], in1=tmp[tz][:, 0:n], op=ALU.mult),
                        reads=[f"tmp{t2}", f"tmp{tz}"], writes=[f"ya{j}_{si}"])
                    if last:
                        if kind == "p":
                            S.add("pool", lambda e, ss_=ss_, j=j: e.tensor_copy(out=skeep[:, j, :], in_=sext[ss_][:, 512:514]),
                                  reads=[sres], writes=[f"skeep{j}"])
                        else:
                            S.add("pool", lambda e, ss_=ss_, j=j: e.tensor_copy(out=skeep_s[:, j, :], in_=sexts[ss_][:, 128:160]),
                                  reads=[sres], writes=[f"skeep_s{j}"])
                    if kind == "p" and si == nps - 1 and not last:
                        S.add("pool", lambda e, ss_=ss_, j=j, npr=npr: e.tensor_copy(out=halo_s[:, j, :], in_=sext[ss_][:, npr:npr + 2]),
                              reads=[f"sext{ss_}_{q}" for q in range(nps)], writes=[f"halo_s{j}"])

                if j >= 4:
                    s4_chunk(2 * (j - 4))
                    s4_chunk(2 * (j - 4) + 1)

            SKEW = 2
            for idx in range(len(jobs) + SKEW):
                if idx < len(jobs):
                    s2_proj(*jobs[idx])
                if idx == len(jobs):
                    s3_chunk(0)
                if idx >= SKEW:
                    s2_conv(*jobs[idx - SKEW])
            if bi == 0:
                hist_setup()

            for j in range(1, NCH):
                s3_chunk(j)

            if last:
                out_T(lambda j: vkeep_s[:, j, :], 128, [f"vkeep_s{j}" for j in range(NCH)], BB[0], ["BB0_0", "BB0_1"],
                      [(ncbs_new_d, 0, 128, "o_ncbs")], extra_reads=["BB1_0", "BB1_1"])

            if next_tiles:
                s1_load(next_tiles, 0)
            for ti in range(min(2, len(tiles))):
                row0, off = tiles[ti]
                sl = ti % 2
                S.add("sp", lambda e, sl=sl, row0=row0: e.dma_start(out=xs[sl][:], in_=x_d[row0:row0 + 128, :]),
                      writes=[f"xs{sl}"], dma=f"xs{sl}")
                S.add("pool", lambda e, sl=sl, row0=row0: e.dma_start(out=pbt[sl][:], in_=p_d[row0:row0 + 128, :]),
                      writes=[f"pbt{sl}", swq()], dma=f"pbt{sl}")

            for j in range(NCH):
                wg5, (ka, kbb, kga, kgb) = wgroup()
                for si, (off, n, kind) in enumerate(sbs):
                    bks = [balloc() for _ in range(4)]
                    pya, pyb, pga, pgb = bks
                    for k in range(NCH):
                        xr = [f"ring{q}" for q in (ka, kbb, kga, kgb)] if k == 0 else []
                        xw = [f"ps{q}" for q in bks] if k == 0 else []
                        S.add("pe", lambda e, pya=pya, ka=ka, k=k, off=off, n=n: e.matmul(
                            ps[:, pya, 0:n], lhsT=ring[ka][:, k, :], rhs=yaT[:, k, off:off + n],
                            start=(k == 0), stop=(k == NCH - 1)),
                            reads=[f"ring{ka}", f"ya{k}_{si}"] + xr, writes=[f"ps{pya}"] + xw)
                    for k in range(NCH):
                        S.add("pe", lambda e, pyb=pyb, kbb=kbb, k=k, off=off, n=n: e.matmul(
                            ps[:, pyb, 0:n], lhsT=ring[kbb][:, k, :], rhs=cbufb[:, k, 2 * off:2 * off + n],
                            start=(k == 0), stop=(k == NCH - 1)),
                            reads=[f"ring{kbb}", f"yb{k}_{si}"], writes=[f"ps{pyb}"])
                    for (b, kw) in ((pga, kga), (pgb, kgb)):
                        for k in range(NCH):
                            S.add("pe", lambda e, b=b, kw=kw, k=k, off=off, n=n: e.matmul(
                                ps[:, b, 0:n], lhsT=ring[kw][:, k, :], rhs=uT[:, k, off:off + n],
                                start=(k == 0), stop=(k == NCH - 1)),
                                reads=[f"ring{kw}"] + ures(off, n), writes=[f"ps{b}"])
                    if si == len(sbs) - 1:
                        wdone(wg5)
                    ta, tb = talloc(), talloc()
                    S.add("act", lambda e, pga=pga, ta=ta, n=n: e.activation(out=tmp[ta][:, 0:n], in_=ps[:, pga, 0:n], func=AF.Sigmoid),
                          reads=[f"ps{pga}"], writes=[f"tmp{ta}"])
                    S.add("act", lambda e, pgb=pgb, tb=tb, n=n: e.activation(out=tmp[tb][:, 0:n], in_=ps[:, pgb, 0:n], func=AF.Sigmoid),
                          reads=[f"ps{pgb}"], writes=[f"tmp{tb}"])
                    S.add("dve", lambda e, pya=pya, ta=ta, n=n: e.tensor_tensor(
                        out=tmp[ta][:, 0:n], in0=ps[:, pya, 0:n], in1=tmp[ta][:, 0:n], op=ALU.mult),
                        reads=[f"ps{pya}", f"tmp{ta}"], writes=[f"tmp{ta}"])
                    S.add("dve", lambda e, pyb=pyb, tb=tb, n=n: e.tensor_tensor(
                        out=tmp[tb][:, 0:n], in0=ps[:, pyb, 0:n], in1=tmp[tb][:, 0:n], op=ALU.mult),
                        reads=[f"ps{pyb}", f"tmp{tb}"], writes=[f"tmp{tb}"])
                    S.add("dve", lambda e, ta=ta, tb=tb, j=j, off=off, n=n: e.tensor_tensor(
                        out=cbufb[:, j, 2 * off + n:2 * off + 2 * n], in0=tmp[ta][:, 0:n], in1=tmp[tb][:, 0:n], op=ALU.add),
                        reads=[f"tmp{ta}", f"tmp{tb}"], writes=[f"m{j}_{si}"])
                    for b in bks:
                        bfree(b)

            def sb_of(off):
                for si, (o, n, kind) in enumerate(sbs):
                    if o <= off < o + n:
                        return si, o, n
                raise AssertionError

            s6 = {}

            def s6_xload(ti):
                row0, off = tiles[ti]
                sl = ti % 2
                S.add("sp", lambda e, sl=sl, row0=row0: e.dma_start(out=xs[sl][:], in_=x_d[row0:row0 + 128, :]),
                      writes=[f"xs{sl}"], dma=f"xs{sl}")

            def s6_pload(ti):
                row0, off = tiles[ti]
                sl = ti % 2
                S.add("pool", lambda e, sl=sl, row0=row0: e.dma_start(out=pbt[sl][:], in_=p_d[row0:row0 + 128, :]),
                      writes=[f"pbt{sl}", swq()], dma=f"pbt{sl}")

            def s6_A(ti):
                row0, off = tiles[ti]
                si, o, n = sb_of(off)
                sl = ti % 2
                if ti >= 2:
                    s6_pload(ti)
                mcol = 2 * o + n + (off - o)
                bm = [balloc(), balloc()]
                for h in range(2):
                    for k in range(NCH):
                        xw = [f"ps{bm[1]}"] if (h == 0 and k == 0) else []
                        S.add("pe", lambda e, h=h, k=k, bm=bm, mcol=mcol: e.matmul(
                            ps[:, bm[h], :], lhsT=cbufb[:, k, mcol:mcol + 128], rhs=wo_t[:, k, 512 * h:512 * h + 512],
                            start=(k == 0), stop=(k == NCH - 1)),
                            reads=[f"m{k}_{si}", f"wo{k // 4}"], writes=[f"ps{bm[h]}"] + xw)
                for h in range(2):
                    S.add("dve", lambda e, h=h, sl=sl, bm=bm: e.tensor_tensor(
                        out=Abuf[sl][:, 512 * h:512 * h + 512], in0=ps[:, bm[h], :], in1=xs[sl][:, 512 * h:512 * h + 512],
                        op=ALU.add), reads=[f"ps{bm[h]}", f"xs{sl}"], writes=[f"A{sl}_{h}"])
                bfree(bm[0])
                bfree(bm[1])
                S.add("act", lambda e, sl=sl: e.activation(out=xb[sl][:], in_=Abuf[sl][:], func=AF.Copy),
                      reads=[f"A{sl}_0", f"A{sl}_1"], writes=[f"xb{sl}"])
                s6[ti] = rms_stats(Abuf[sl][:], [f"A{sl}_0", f"A{sl}_1"], xs[sl][:], [f"xs{sl}"])
                if ti + 2 < len(tiles):
                    s6_xload(ti + 2)

            def s6_B(ti):
                sl = ti % 2
                bu = balloc()
                for k in range(NCH):
                    S.add("pe", lambda e, sl=sl, k=k, bu=bu: e.transpose(out=psb[:, bu, k * 128:(k + 1) * 128],
                                                                          in_=xb[sl][:, k * 128:(k + 1) * 128], identity=idb[:]),
                          reads=[f"xb{sl}", "idb"], writes=[f"ps{bu}"])
                S.add("dve", lambda e, bu=bu: e.tensor_tensor(
                    out=u2T[:], in0=psb[:, bu, :].rearrange("p (k t) -> p k t", t=128),
                    in1=vecs[:, V_GPLE:V_GPLE + 8].unsqueeze(2).to_broadcast([128, NCH, 128]), op=ALU.mult),
                    reads=[f"ps{bu}", "vecs"], writes=["u2T"])
                bfree(bu)
                bp = balloc()
                for q in range(2):
                    S.add("pe", lambda e, sl=sl, q=q, bp=bp: e.transpose(out=psb[:, bp, q * 128:(q + 1) * 128],
                                                                          in_=pbt[sl][:, q * 128:(q + 1) * 128], identity=idb[:]),
                          reads=[f"pbt{sl}", "idb"], writes=[f"ps{bp}"])
                S.add("act", lambda e, bp=bp: e.activation(out=pT[:], in_=psb[:, bp, 0:256].rearrange("p (k t) -> p k t", t=128),
                                                           func=AF.Copy),
                      reads=[f"ps{bp}"], writes=["pT"])
                bfree(bp)

            def s6_C(ti):
                row0, off = tiles[ti]
                sl = ti % 2
                bg = [balloc(), balloc()]
                be = [balloc(), balloc()]
                for h in range(2):
                    for k in range(NCH):
                        xw = [f"ps{q}" for q in bg + be] if (h == 0 and k == 0) else []
                        xr = ["pT", "wpe"] if (h == 0 and k == 0) else []
                        S.add("pe", lambda e, h=h, k=k, bg=bg: e.matmul(
                            ps[:, bg[h], :], lhsT=u2T[:, k, :], rhs=wpg_t[:, k, 512 * h:512 * h + 512],
                            start=(k == 0), stop=(k == NCH - 1)),
                            reads=["u2T", f"wpg{k // 4}"] + xr, writes=[f"ps{bg[h]}"] + xw)
                for h in range(2):
                    for q in range(2):
                        S.add("pe", lambda e, h=h, q=q, be=be: e.matmul(
                            ps[:, be[h], :], lhsT=pT[:, q, :], rhs=wpe_t[:, q, 512 * h:512 * h + 512],
                            start=(q == 0), stop=(q == 1)),
                            reads=["pT", "wpe"], writes=[f"ps{be[h]}"])
                rs2_ap, rs2_res = s6[ti]
                bs = ti % 2
                Bt = BB[bs]
                for h in range(2):
                    S.add("act", lambda e, h=h, bg=bg, rs2_ap=rs2_ap, Bt=Bt: e.activation(
                        out=Bt[:, 512 * h:512 * h + 512], in_=ps[:, bg[h], :], func=AF.Sigmoid, scale=rs2_ap),
                        reads=[f"ps{bg[h]}", rs2_res], writes=[f"BB{bs}_{h}"])
                    S.add("dve", lambda e, h=h, be=be, Bt=Bt: e.tensor_tensor(
                        out=Bt[:, 512 * h:512 * h + 512], in0=ps[:, be[h], :], in1=Bt[:, 512 * h:512 * h + 512], op=ALU.mult),
                        reads=[f"ps{be[h]}", f"BB{bs}_{h}"], writes=[f"BB{bs}_{h}"])
                for b in bg + be:
                    bfree(b)
                S.add("dve", lambda e, sl=sl, Bt=Bt: e.tensor_tensor(out=Bt, in0=Bt, in1=Abuf[sl][:], op=ALU.add),
                      reads=[f"BB{bs}_0", f"BB{bs}_1", f"A{sl}_0", f"A{sl}_1"], writes=[f"BB{bs}_0", f"BB{bs}_1"])

            def s6_C2(ti):
                pump(2)
                row0, off = tiles[ti]
                bs = ti % 2
                Bt = BB[bs]
                rs_ap, rs_res = rms_stats(Bt, [f"BB{bs}_0", f"BB{bs}_1"], yt[:], ["yt"])
                S.add("dve", lambda e, rs_ap=rs_ap, Bt=Bt: e.scalar_tensor_tensor(
                    out=yt[:], in0=Bt, scalar=rs_ap, in1=gfin[:], op0=ALU.mult, op1=ALU.mult),
                    reads=[f"BB{bs}_0", f"BB{bs}_1", rs_res, "gfin"], writes=["yt"])
                S.add("sp", lambda e, row0=row0: e.dma_start(out=y_d[row0:row0 + 128, :], in_=yt[:]),
                      reads=["yt"], dma="yt")

            nt = len(tiles)
            s6_A(0)
            if nt > 1:
                s6_A(1)
            s6_B(0)
            for ti in range(nt):
                hoist = bool(next_tiles) and ti == nt - 1
                if hoist:
                    s1_stats(next_tiles, 0)
                    s1_stats(next_tiles, 1)
                s6_C(ti)
                if hoist:
                    s1_rest(next_tiles, 0)
                    if len(next_tiles) > 3:
                        s1_load(next_tiles, 3)
                if ti + 1 < nt:
                    s6_B(ti + 1)
                if ti + 2 < nt:
                    s6_A(ti + 2)
                    if ti + 2 == nt - 1 and next_tiles:
                        s1_load(next_tiles, 1)
                        s1_load(next_tiles, 2)
                s6_C2(ti)

        out_T(lambda j: vkeep[:, j, :], 32, [f"vkeep{j}" for j in range(NCH)], Abuf[0], ["A0_0", "A0_1"],
              [(ncbp_d, 2, 32, "o_ncbp")])
        out_T(lambda j: skeep[:, j, :], 2, [f"skeep{j}" for j in range(NCH)], Abuf[1], ["A1_0", "A1_1"],
              [(ncap_d, 0, 2, "o_ncap")])
        out_T(lambda j: skeep_s[:, j, :], 32, [f"skeep_s{j}" for j in range(NCH)], xs[0], ["xs0", "xs0"],
              [(ncas_d, 0, 32, "o_ncas")])

        S.emit()
    return nc


_CACHE = {}


def kernel(x_prompt, x_sample, state_conv_a, state_conv_b, p_prompt, p_sample,
           g_mix, w_in, w_conv_a, w_out_a, w_conv_b, b_conv_b, ln_g, ln_b,
           w_out_b, w_o, w_pe, g_ple, w_pg, g_final):
    f = lambda a: np.ascontiguousarray(np.asarray(a, dtype=np.float32))
    x_prompt, x_sample = f(x_prompt), f(x_sample)
    state_conv_a, state_conv_b = f(state_conv_a), f(state_conv_b)
    p_prompt, p_sample = f(p_prompt), f(p_sample)
    NC = 8
    vecs = np.zeros((128, NV), np.float32)

    def fm(v):
        return f(v).reshape(8, 128).T

    vecs[:, V_GMIX:V_GMIX + 8] = fm(g_mix[0])
    vecs[:, V_GPLE:V_GPLE + 8] = fm(g_ple[0])
    vecs[:, V_BCONV:V_BCONV + 8] = fm(b_conv_b[0])
    vecs[:, V_LNG:V_LNG + 8] = fm(ln_g[0])
    vecs[:, V_LNB:V_LNB + 8] = fm(ln_b[0])
    wca = f(w_conv_a[0])
    vecs[:, V_WCA:V_WCA + 24] = wca.reshape(3, 8, 128).transpose(2, 1, 0).reshape(128, 24)
    wcb = f(w_conv_b[0])
    wpad = np.concatenate([wcb, np.zeros((1, D), np.float32)], axis=0)
    wl = wpad.reshape(16, 2, 8, 2, 64).transpose(1, 4, 2, 3, 0)
    vecs[:, V_WCB:V_WCB + 256] = wl.reshape(128, 256)
    vecs[:, V_MASK:V_MASK + 64] = np.tile(np.eye(64, dtype=np.float32), (2, 1))
    gfin = np.ascontiguousarray(np.broadcast_to(f(g_final)[None, :], (128, D)))
    ident = np.eye(128, dtype=np.float32)
    shared = dict(w_in=f(w_in[0]), w_oa=f(w_out_a[0]), w_ob=f(w_out_b[0]), w_o=f(w_o[0]), w_pg=f(w_pg[0]),
                  w_pe=f(w_pe[0]), vecs=vecs, gfin=gfin, ident=ident)
    in_maps = []
    for c in range(NC):
        s0 = 16 * c
        xs_ = x_sample[s0:s0 + 16].transpose(1, 0, 2).reshape(128, D)
        ps_ = p_sample[0, s0:s0 + 16].transpose(1, 0, 2).reshape(128, 256)
        m = dict(shared)
        m["x"] = np.ascontiguousarray(np.concatenate([x_prompt[c], xs_], axis=0))
        m["p"] = np.ascontiguousarray(np.concatenate([p_prompt[0, c], ps_], axis=0))
        m["sa"] = np.ascontiguousarray(state_conv_a[0, s0:s0 + 16].transpose(1, 0, 2).reshape(32, D))
        m["sb"] = np.ascontiguousarray(state_conv_b[0, s0:s0 + 16].transpose(1, 0, 2).reshape(480, D))
        in_maps.append(m)
    if "nc" not in _CACHE:
        _CACHE["nc"] = build_program()
    res = run_bass_kernel_spmd(_CACHE["nc"], in_maps, core_ids=list(range(NC)))
    R = res.results
    y_prompt = np.stack([R[c]["y"][:2048] for c in range(NC)], axis=0)
    y_sample = np.concatenate(
        [R[c]["y"][2048:].reshape(8, 16, D).transpose(1, 0, 2) for c in range(NC)], axis=0)
    nca_p = np.stack([R[c]["nca_p"] for c in range(NC)], axis=0)[None]
    ncb_p = np.stack([R[c]["ncb_p"] for c in range(NC)], axis=0)[None]
    nca_s = np.concatenate([R[c]["nca_s"].reshape(2, 16, D).transpose(1, 0, 2) for c in range(NC)], axis=0)[None]
    ncb_s = np.concatenate(
        [np.concatenate([R[c]["ncb_s_hist"].reshape(22, 16, D).transpose(1, 0, 2),
                         R[c]["ncb_s_new"].reshape(8, 16, D).transpose(1, 0, 2)], axis=1) for c in range(NC)],
        axis=0)[None]
    out = (y_prompt, y_sample, nca_p, ncb_p, nca_s, ncb_s)
    return tuple(np.ascontiguousarray(o, dtype=np.float32) for o in out)
```

```python
import contextlib
import numpy as np
import concourse.bass as bass
import concourse.mybir as mybir
from concourse.bass_utils import run_bass_kernel_spmd

F32 = mybir.dt.float32
BF16 = mybir.dt.bfloat16
AF = mybir.ActivationFunctionType
ALU = mybir.AluOpType

D = 1024
NCH = 8
KB = 31
EPS = 1e-6
LN_EPS = 1e-5
NTOK = 2176
NTB = 768

V_GMIX, V_GPLE, V_BCONV, V_LNG, V_LNB = 0, 8, 16, 24, 32
V_WCA = 40
V_WCB = 64
V_MASK = V_WCB + 256
NV = V_MASK + 64

BLOCKS = [
    dict(p0=0, npr=768, sbs=[(0, 512, "p"), (512, 256, "p")]),
    dict(p0=768, npr=768, sbs=[(0, 512, "p"), (512, 256, "p")]),
    dict(p0=1536, npr=512, sbs=[(0, 512, "p"), (512, 128, "s")]),
]


class Sched:
    def __init__(self, nc):
        self.nc = nc
        self.ops = []
        self.last_writer = {}
        self.readers = {}
        self.chan_count = {}

    def add(self, eng, fn, reads=(), writes=(), dma=None):
        idx = len(self.ops)
        chan = eng if dma is None else ("dma", dma)
        deps = set()
        raw = set()
        for r in reads:
            w = self.last_writer.get(r)
            if w is not None:
                deps.add(w)
                raw.add(w)
        for r in writes:
            w = self.last_writer.get(r)
            if w is not None:
                deps.add(w)
            for rd in self.readers.get(r, {}).values():
                deps.add(rd)
        keep = set()
        for d in deps:
            o = self.ops[d]
            if o["chan"] == eng and dma is None:
                if eng == "pe":
                    continue
            keep.add(d)
        for r in reads:
            self.readers.setdefault(r, {})[chan] = idx
        for r in writes:
            self.last_writer[r] = idx
            self.readers[r] = {}
        seq = None
        if dma is not None:
            seq = self.chan_count.get(chan, 0) + 1
            self.chan_count[chan] = seq
        self.ops.append(dict(eng=eng, chan=chan, fn=fn, deps=keep, dma=dma, seq=seq,
                             signal=dma is not None))
        for d in keep:
            self.ops[d]["signal"] = True
        return idx

    def emit(self, final_wait_eng="sp"):
        nc = self.nc
        ops = self.ops
        cnt = {}
        for o in ops:
            if o["dma"] is None and o["signal"]:
                cnt[o["chan"]] = cnt.get(o["chan"], 0) + 1
                o["seq"] = cnt[o["chan"]]
        chans = []
        for o in ops:
            if o["signal"] and o["chan"] not in chans:
                chans.append(o["chan"])
        with contextlib.ExitStack() as st:
            sems = {}
            for c in chans:
                nm = "s_" + (c if isinstance(c, str) else "d_" + str(c[1]))
                sems[c] = st.enter_context(nc.semaphore(nm))
            block = st.enter_context(nc.Block())
            final = {}
            for o in ops:
                if o["dma"] is not None:
                    final[o["chan"]] = o["seq"] * 16

            def run(engname, e):
                waited = {}
                for o in ops:
                    if o["eng"] != engname:
                        continue
                    need = {}
                    for d in o["deps"]:
                        od = ops[d]
                        v = od["seq"] * (16 if od["dma"] is not None else 1)
                        if v > need.get(od["chan"], 0):
                            need[od["chan"]] = v
                    for c, v in need.items():
                        if waited.get(c, 0) >= v:
                            continue
                        e.wait_ge(sems[c], v)
                        waited[c] = v
                    ins = o["fn"](e)
                    if o["signal"]:
                        ins.then_inc(sems[o["chan"]], 16 if o["dma"] is not None else 1)
                if engname == final_wait_eng:
                    for c, v in final.items():
                        if waited.get(c, 0) < v:
                            e.wait_ge(sems[c], v)

            @block.tensor
            def _(e):
                run("pe", e)

            @block.scalar
            def _(e):
                run("act", e)

            @block.vector
            def _(e):
                run("dve", e)

            @block.gpsimd
            def _(e):
                run("pool", e)

            @block.sync
            def _(e):
                run("sp", e)


def build_program():
    nc = bass.Bass("TRN2", target_bir_lowering=False)

    def din(name, shape):
        return nc.dram_tensor(name, shape, F32, kind="ExternalInput").ap()

    def dout(name, shape):
        return nc.dram_tensor(name, shape, F32, kind="ExternalOutput").ap()

    x_d = din("x", [NTOK, D])
    p_d = din("p", [NTOK, 256])
    sa_d = din("sa", [32, D])
    sb_d = din("sb", [480, D])
    w_in = din("w_in", [D, 9 * D])
    w_oa = din("w_oa", [D, D])
    w_ob = din("w_ob", [D, D])
    w_o = din("w_o", [D, D])
    w_pg = din("w_pg", [D, D])
    w_pe = din("w_pe", [256, D])
    vecs_d = din("vecs", [128, NV])
    gfin_d = din("gfin", [128, D])
    ident_d = din("ident", [128, 128])

    y_d = dout("y", [NTOK, D])
    ncap_d = dout("nca_p", [2, D])
    ncbp_d = dout("ncb_p", [30, D])
    ncas_d = dout("nca_s", [32, D])
    ncbs_new_d = dout("ncb_s_new", [128, D])
    ncbs_hist_d = dout("ncb_s_hist", [352, D])

    with contextlib.ExitStack() as st:
        def sb(name, shape, dt):
            return st.enter_context(nc.sbuf_tensor("t_" + name, shape, dt))

        idf = sb("idf", [128, 128], F32)
        idb = sb("idb", [128, 128], BF16)
        onesf = sb("onesf", [128, 128], F32)
        vecs = sb("vecs", [128, NV], F32)
        gfin = sb("gfin", [128, D], F32)
        mhalf = sb("mhalf", [128, 1], F32)
        histb = sb("histb", [128, NCH, 480], BF16)
        hista = sb("hista", [128, NCH, 32], F32)
        halo_v = sb("halo_v", [128, NCH, 30], BF16)
        halo_s = sb("halo_s", [128, NCH, 2], F32)
        uT = sb("uT", [128, NCH, NTB], BF16)
        yaT = sb("yaT", [128, NCH, NTB], BF16)
        cbuf = sb("cbuf", [128, NCH, NTB], F32)
        cbufb = cbuf.bitcast(BF16)
        NRING = 8
        ring = [sb(f"ring{i}", [128, NCH, 128], BF16) for i in range(NRING)]
        wo_t = sb("wo_t", [128, NCH, D], BF16)
        wpg_t = sb("wpg_t", [128, NCH, D], BF16)
        wpe_t = sb("wpe_t", [128, 2, D], BF16)
        Wp = [sb(f"Wp{i}", [128, 32, 64], BF16) for i in range(2)]
        RL = 608
        NRS = 3
        Rb = [[sb(f"R{i}_{g}", [128, RL], BF16) for g in range(2)] for i in range(NRS)]
        vext = [sb(f"vext{i}", [128, 30 + NTB + 2], BF16) for i in range(2)]
        vexts = [sb(f"vexts{i}", [128, 624], BF16) for i in range(2)]
        sext = [sb(f"sext{i}", [128, 2 + NTB], F32) for i in range(1)]
        sexts = [sb(f"sexts{i}", [128, 160], F32) for i in range(1)]
        xs = [sb(f"xs{i}", [128, D], F32) for i in range(3)]
        xb = [sb(f"xb{i}", [128, D], BF16) for i in range(2)]
        ssq = sb("ssq", [128, 64], F32)
        msq = sb("msq", [128, 64], F32)
        rsd = sb("rsd", [128, 64], F32)
        NTMP = 6
        tmp = [sb(f"tmp{i}", [128, 512], F32) for i in range(NTMP)]
        accs = sb("accs", [128, NTB], F32)
        accq = sb("accq", [128, NTB], F32)
        rstdB = sb("rstdB", [128, NTB], F32)
        nmrB = sb("nmrB", [128, NTB], F32)
        Abuf = [sb(f"Abuf{i}", [128, D], F32) for i in range(2)]
        Bbuf = sb("Bbuf", [128, D], F32)
        yt = sb("yt", [128, D], F32)
        u2T = sb("u2T", [128, NCH, 128], BF16)
        pbt = [sb(f"pbt{i}", [128, 256], BF16) for i in range(2)]
        pT = sb("pT", [128, 2, 128], BF16)
        vkeep = sb("vkeep", [128, NCH, 32], F32)
        skeep = sb("skeep", [128, NCH, 2], F32)
        vkeep_s = sb("vkeep_s", [128, NCH, 128], F32)
        BB = [Bbuf[:, :], vkeep_s.rearrange("p a b -> p (a b)")]
        skeep_s = sb("skeep_s", [128, NCH, 32], F32)
        ps = st.enter_context(nc.psum_tensor("ps", [128, 8, 512], F32))
        psb = ps.bitcast(BF16)

        S = Sched(nc)

        free_banks = list(range(8))

        def balloc():
            return free_banks.pop(0)

        def bfree(b):
            free_banks.append(b)

        tmp_ctr = [0]

        def talloc():
            k = tmp_ctr[0] % NTMP
            tmp_ctr[0] += 1
            return k

        ring_ctr = [0]

        SWQ = 4
        swq_ctr = [0]

        def swq():
            r = f"swq{swq_ctr[0] % SWQ}"
            swq_ctr[0] += 1
            return r

        WL, WG = [], []

        def add_group(srcs):
            WG.append(list(range(len(WL), len(WL) + len(srcs))))
            for sap in srcs:
                WL.append((sap, len(WG) - 1))

        def wcol(c0, j):
            return w_in[:, c0 * D + 128 * j:c0 * D + 128 * j + 128]

        for _bi in range(len(BLOCKS)):
            for j in range(NCH):
                add_group([wcol(4, j), wcol(5, j)])
            for j in range(NCH):
                add_group([wcol(1, j), wcol(2, j), wcol(0, j), wcol(3, j)])
                if j >= 4:
                    add_group([wcol(6, 2 * (j - 4))])
                    add_group([wcol(6, 2 * (j - 4) + 1)])
            for j in range(NCH):
                add_group([w_oa[:, 128 * j:128 * j + 128], w_ob[:, 128 * j:128 * j + 128], wcol(7, j), wcol(8, j)])
        wst = dict(nl=0, ng=0, done=set())

        def pump(maxn=4):
            nem = 0
            while nem < maxn and wst["nl"] < len(WL) and (wst["nl"] < NRING or WL[wst["nl"] - NRING][1] in wst["done"]):
                nem += 1
                t = wst["nl"]
                wst["nl"] += 1
                k = t % NRING
                src = WL[t][0]
                S.add("pool", lambda e, k=k, src=src: e.dma_start(
                    out=ring[k][:, :, :], in_=src.rearrange("(k p) m -> p k m", p=128)),
                    writes=[f"ring{k}", swq()], dma=f"ring{k}")

        def wgroup():
            g = wst["ng"]
            wst["ng"] += 1
            if not all(t < wst["nl"] for t in WG[g]):
                pump(NRING)
            assert all(t < wst["nl"] for t in WG[g]), "weight tile not prefetched"
            return g, [t % NRING for t in WG[g]]

        def wdone(g):
            wst["done"].add(g)
            pump()

        stat_ctr = [0]

        def rms_stats(src_ap, src_res, junk_ap, junk_res):
            c = stat_ctr[0] % 64
            stat_ctr[0] += 1
            S.add("act", lambda e, c=c: e.activation(out=junk_ap, in_=src_ap, func=AF.Square,
                                                     accum_out=ssq[:, c:c + 1]),
                  reads=src_res, writes=list(junk_res) + [f"ssq{c}"])
            S.add("pool", lambda e, c=c: e.tensor_scalar(out=msq[:, c:c + 1], in0=ssq[:, c:c + 1],
                                                         scalar1=1.0 / D, scalar2=EPS,
                                                         op0=ALU.mult, op1=ALU.add),
                  reads=[f"ssq{c}"], writes=[f"msq{c}"])
            S.add("pool", lambda e, c=c: e.tensor_tensor(out=rsd[:, c:c + 1], in0=msq[:, c:c + 1],
                                                         in1=mhalf[:, 0:1], op=ALU.pow),
                  reads=[f"msq{c}", "mhalf"], writes=[f"rsd{c}"])
            return rsd[:, c:c + 1], f"rsd{c}"

        def vcol(off, j):
            return vecs[:, off + j:off + j + 1]

        S.add("sp", lambda e: e.dma_start(out=idf[:], in_=ident_d), writes=["idf"], dma="idf")
        S.add("sp", lambda e: e.dma_start(out=vecs[:], in_=vecs_d), writes=["vecs"], dma="vecs")
        S.add("sp", lambda e: e.dma_start(out=gfin[:], in_=gfin_d), writes=["gfin"], dma="gfin")
        S.add("dve", lambda e: e.tensor_copy(out=idb[:], in_=idf[:]), reads=["idf"], writes=["idb"])
        S.add("pool", lambda e: e.memset(onesf[:], 1.0), writes=["onesf"])
        for i in range(2):
            S.add("pool", lambda e, i=i: e.memset(vext[i][:], 0.0), writes=[f"vext{i}h", f"vext{i}_0", f"vext{i}_1"])
            S.add("pool", lambda e, i=i: e.memset(vexts[i][:], 0.0), writes=[f"vexts{i}h", f"vexts{i}n"])
        S.add("pool", lambda e: e.memset(mhalf[:], -0.5), writes=["mhalf"])
        S.add("pool", lambda e: e.memset(halo_v[:], 0.0), writes=[f"halo_v{j}" for j in range(NCH)])
        S.add("pool", lambda e: e.memset(halo_s[:], 0.0), writes=[f"halo_s{j}" for j in range(NCH)])
        pump()
        RESIDENT = []
        for h in range(2):
            RESIDENT.append(lambda h=h: S.add("pool", lambda e: e.dma_start(
                out=wo_t[:, 4 * h:4 * h + 4, :],
                in_=w_o[512 * h:512 * h + 512, :].rearrange("(k p) m -> p k m", p=128)),
                writes=[f"wo{h}", swq()], dma=f"wo{h}"))
        for h in range(2):
            RESIDENT.append(lambda h=h: S.add("pool", lambda e: e.dma_start(
                out=wpg_t[:, 4 * h:4 * h + 4, :],
                in_=w_pg[512 * h:512 * h + 512, :].rearrange("(k p) m -> p k m", p=128)),
                writes=[f"wpg{h}", swq()], dma=f"wpg{h}"))
        RESIDENT.append(lambda: S.add("pool", lambda e: e.dma_start(
            out=wpe_t[:], in_=w_pe.rearrange("(k p) m -> p k m", p=128)),
            writes=["wpe", swq()], dma="wpe"))

        def hist_setup():
            rows_list = [(0, 128), (128, 128), (256, 128), (384, 96)]
            for ti, (r0, nr) in enumerate(rows_list):
                sl = ti % 2
                S.add("sp", lambda e, sl=sl, r0=r0, nr=nr: e.dma_start(out=Abuf[sl][0:nr, :], in_=sb_d[r0:r0 + nr, :]),
                      writes=[f"A{sl}_0", f"A{sl}_1"], dma=f"hst{sl}")
                b0, b1 = balloc(), balloc()
                for k in range(NCH):
                    b = b0 if k < 4 else b1
                    S.add("pe", lambda e, sl=sl, k=k, b=b, nr=nr: e.transpose(
                        out=ps[:, b, (k % 4) * 128:(k % 4) * 128 + nr], in_=Abuf[sl][0:nr, k * 128:(k + 1) * 128],
                        identity=idf[0:nr, 0:nr]),
                        reads=[f"A{sl}_0", f"A{sl}_1", "idf"], writes=[f"ps{b}"])
                for h, b in enumerate((b0, b1)):
                    S.add("act", lambda e, h=h, b=b, r0=r0, nr=nr: e.activation(
                        out=histb[:, 4 * h:4 * h + 4, r0:r0 + nr],
                        in_=ps[:, b, :].rearrange("p (k t) -> p k t", t=128)[:, :, 0:nr], func=AF.Copy),
                        reads=[f"ps{b}"], writes=[f"histb{ti}_{h}"])
                bfree(b0)
                bfree(b1)
            S.add("sp", lambda e: e.dma_start(out=Abuf[0][0:32, :], in_=sa_d), writes=["A0_0", "A0_1"], dma="hst0")
            b0 = balloc()
            for k in range(NCH):
                S.add("pe", lambda e, k=k, b0=b0: e.transpose(out=ps[:, b0, k * 32:(k + 1) * 32],
                                                              in_=Abuf[0][0:32, k * 128:(k + 1) * 128],
                                                              identity=idf[0:32, 0:32]),
                      reads=["A0_0", "A0_1", "idf"], writes=[f"ps{b0}"])
            S.add("act", lambda e, b0=b0: e.activation(out=hista[:], in_=ps[:, b0, 0:256].rearrange("p (k t) -> p k t", t=32),
                                                       func=AF.Copy),
                  reads=[f"ps{b0}"], writes=["hista"])
            bfree(b0)
            S.add("sp", lambda e: e.dma_start(out=ncbs_hist_d, in_=sb_d[128:480, :]), dma="ncbs_hist")

        HISTB_RES = [f"histb{ti}_{h}" for ti in range(4) for h in range(2)]

        def tiles_of(blk):
            t = [(blk["p0"] + 128 * i, 128 * i) for i in range(blk["npr"] // 128)]
            if any(k == "s" for _, _, k in blk["sbs"]):
                t.append((2048, blk["npr"]))
            return t

        s1st = {}

        s1_loaded = set()

        def s1_load(tl, ti):
            s1_loaded.add((tl[0][0], ti))
            row0, off = tl[ti]
            xsl = (ti + 2) % 3
            S.add("sp", lambda e, xsl=xsl, row0=row0: e.dma_start(out=xs[xsl][:], in_=x_d[row0:row0 + 128, :]),
                  writes=[f"xs{xsl}"], dma=f"xs{xsl}")

        s1_sdone, s1_rdone = set(), set()

        def s1_stats(tl, ti):
            s1_sdone.add((tl[0][0], ti))
            xsl = (ti + 2) % 3
            sl = ti % 2
            s1st[ti] = rms_stats(xs[xsl][:], [f"xs{xsl}"], xb[sl][:], [f"xb{sl}"])

        def s1_rest(tl, ti):
            s1_rdone.add((tl[0][0], ti))
            pump(2)
            row0, off = tl[ti]
            xsl, sl = (ti + 2) % 3, ti % 2
            rs_ap, rs_res = s1st[ti]
            S.add("act", lambda e, sl=sl, xsl=xsl, rs_ap=rs_ap: e.activation(out=xb[sl][:], in_=xs[xsl][:], func=AF.Copy,
                                                                             scale=rs_ap),
                  reads=[f"xs{xsl}", rs_res], writes=[f"xb{sl}"])
            b = balloc()
            for k in range(NCH):
                S.add("pe", lambda e, sl=sl, k=k, b=b: e.transpose(out=psb[:, b, k * 128:(k + 1) * 128],
                                                                    in_=xb[sl][:, k * 128:(k + 1) * 128],
                                                                    identity=idb[:]),
                      reads=[f"xb{sl}", "idb"], writes=[f"ps{b}"])
            S.add("dve", lambda e, b=b, off=off: e.tensor_tensor(
                out=uT[:, :, off:off + 128], in0=psb[:, b, :].rearrange("p (k t) -> p k t", t=128),
                in1=vecs[:, V_GMIX:V_GMIX + 8].unsqueeze(2).to_broadcast([128, NCH, 128]), op=ALU.mult),
                reads=[f"ps{b}", "vecs"], writes=[f"uT{off // 128}"])
            bfree(b)

        def out_T(src_fn, nrow, res_list, stage, stage_res, dsts, extra_reads=()):
            b0, b1 = balloc(), balloc()
            for j in range(NCH):
                b = b0 if j < 4 else b1
                S.add("pe", lambda e, j=j, b=b: e.transpose(out=ps[0:nrow, b, (j % 4) * 128:(j % 4) * 128 + 128],
                                                             in_=src_fn(j), identity=idf[:]),
                      reads=[res_list[j], "idf"] + list(extra_reads), writes=[f"ps{b}"])
            for h, b in enumerate((b0, b1)):
                S.add("act", lambda e, h=h, b=b: e.activation(out=stage[0:nrow, 512 * h:512 * h + 512], in_=ps[0:nrow, b, :],
                                                              func=AF.Copy),
                      reads=[f"ps{b}"], writes=[stage_res[h]])
            bfree(b0)
            bfree(b1)
            for (dst, r0, r1, key) in dsts:
                S.add("sp", lambda e, dst=dst, r0=r0, r1=r1: e.dma_start(out=dst, in_=stage[r0:r1, :]),
                      reads=stage_res, dma=key)

        for bi, blk in enumerate(BLOCKS):
            p0, npr, sbs = blk["p0"], blk["npr"], blk["sbs"]
            last = bi == len(BLOCKS) - 1
            has_s = any(k == "s" for _, _, k in sbs)
            tiles = tiles_of(blk)
            next_tiles = tiles_of(BLOCKS[bi + 1]) if not last else []

            def ures(off, n):
                return [f"uT{t}" for t in range(off // 128, (off + n) // 128)]

            for ti in range(min(3, len(tiles))):
                if (tiles[0][0], ti) not in s1_loaded:
                    s1_load(tiles, ti)
            p0k = tiles[0][0]
            for ti in range(len(tiles) + 1):
                if ti < len(tiles) and (p0k, ti) not in s1_sdone:
                    s1_stats(tiles, ti)
                if ti >= 1:
                    if (p0k, ti - 1) not in s1_rdone:
                        s1_rest(tiles, ti - 1)
                    if ti + 2 < len(tiles) and (p0k, ti + 2) not in s1_loaded:
                        s1_load(tiles, ti + 2)

            jobs = [(j, si) for j in range(NCH) for si in range(len(sbs))]
            s2state = {}
            s2rs = {}
            s2_jobctr = [0]

            def diag_build(j):
                ds = j % 2
                S.add("dve", lambda e, ds=ds, j=j: e.tensor_tensor(
                    out=Wp[ds][:], in0=vecs[:, V_MASK:V_MASK + 64].unsqueeze(1).to_broadcast([128, 32, 64]),
                    in1=vecs[:, V_WCB + 32 * j:V_WCB + 32 * j + 32].unsqueeze(2).to_broadcast([128, 32, 64]),
                    op=ALU.mult), reads=["vecs"], writes=[f"diag{ds}"])

            def s2_proj(j, si):
                off, n, kind = sbs[si]
                if si == 0:
                    wg_, (kv, kg) = wgroup()
                    if bi == 0 and j < len(RESIDENT):
                        RESIDENT[j]()
                    ds = j % 2
                    if j < 2:
                        diag_build(j)
                    vs = j % 2
                    S.add("pool", lambda e, vs=vs, j=j: e.tensor_copy(out=vext[vs][:, 0:30], in_=halo_v[:, j, :]),
                          reads=[f"halo_v{j}"], writes=[f"vext{vs}h"])
                    if has_s:
                        S.add("pool", lambda e, vs=vs, j=j: e.tensor_copy(out=vexts[vs][:, 0:480], in_=histb[:, j, :]),
                              reads=HISTB_RES, writes=[f"vexts{vs}h"])
                    s2state[j] = (kv, kg, ds, vs, wg_)
                kv, kg, ds, vs, wg_ = s2state[j]
                bv, bg = balloc(), balloc()
                first = True
                for (b, kw) in ((bv, kv), (bg, kg)):
                    for k in range(NCH):
                        xr = [f"ring{kv}", f"ring{kg}"] if first else []
                        xw = [f"ps{bv}", f"ps{bg}"] if first else []
                        first = False
                        S.add("pe", lambda e, b=b, kw=kw, k=k, off=off, n=n: e.matmul(
                            ps[:, b, 0:n], lhsT=ring[kw][:, k, :], rhs=uT[:, k, off:off + n],
                            start=(k == 0), stop=(k == NCH - 1)),
                            reads=[f"ring{kw}"] + ures(off, n) + xr, writes=[f"ps{b}"] + xw)
                if si == len(sbs) - 1:
                    wdone(wg_)
                tg = talloc()
                S.add("act", lambda e, bg=bg, tg=tg, n=n: e.activation(out=tmp[tg][:, 0:n], in_=ps[:, bg, 0:n],
                                                                       func=AF.Sigmoid),
                      reads=[f"ps{bg}"], writes=[f"tmp{tg}"])
                if kind == "p":
                    vdst, vres = vext[vs][:, 30 + off:30 + off + n], f"vext{vs}_{si}"
                else:
                    vdst, vres = vexts[vs][:, 480:608], f"vexts{vs}n"
                S.add("dve", lambda e, bv=bv, tg=tg, n=n, vdst=vdst: e.tensor_tensor(
                    out=vdst, in0=ps[:, bv, 0:n], in1=tmp[tg][:, 0:n], op=ALU.mult),
                    reads=[f"ps{bv}", f"tmp{tg}"], writes=[vres])
                rs = s2_jobctr[0] % NRS
                s2_jobctr[0] += 1
                s2rs[(j, si)] = rs
                if kind == "p":
                    rr = [f"vext{vs}h"] + [f"vext{vs}_{q}" for q in range(si + 1)]
                    if si + 1 < len(sbs) and sbs[si + 1][2] == "p":
                        rr.append(f"vext{vs}_{si + 1}")
                else:
                    rr = [f"vexts{vs}h", f"vexts{vs}n"]
                for g in range(2):
                    for jj in range(2):
                        if kind == "p":
                            src = vext[vs][64 * g:64 * g + 64, off + jj:off + jj + n + 30]
                            dst = Rb[rs][g][64 * jj:64 * jj + 64, 0:n + 30]
                        else:
                            src = vexts[vs][64 * g:64 * g + 64, 16 * jj:16 * jj + 608]
                            dst = Rb[rs][g][64 * jj:64 * jj + 64, 0:608]
                        S.add("sp", lambda e, src=src, dst=dst: e.dma_start(out=dst, in_=src),
                              reads=rr, writes=[f"R{rs}_{g}_{jj}"], dma=f"R{rs}_{g}_{jj}")
                if last:
                    if kind == "p":
                        S.add("dve", lambda e, bv=bv, tg=tg, j=j: e.tensor_tensor(
                            out=vkeep[:, j, :], in0=ps[:, bv, 480:512], in1=tmp[tg][:, 480:512], op=ALU.mult),
                            reads=[f"ps{bv}", f"tmp{tg}"], writes=[f"vkeep{j}"])
                    else:
                        S.add("dve", lambda e, bv=bv, tg=tg, j=j: e.tensor_tensor(
                            out=vkeep_s[:, j, :], in0=ps[:, bv, 0:128], in1=tmp[tg][:, 0:128], op=ALU.mult),
                            reads=[f"ps{bv}", f"tmp{tg}"], writes=[f"vkeep_s{j}", "BB1_0", "BB1_1"])
                bfree(bv)
                bfree(bg)

            def s2_conv(j, si):
                off, n, kind = sbs[si]
                kv, kg, ds, vs, wg_ = s2state[j]
                bc = balloc()
                if kind == "p":
                    rres = [f"vext{vs}h"] + [f"vext{vs}_{q}" for q in range(si + 1)]
                else:
                    rres = [f"vexts{vs}h", f"vexts{vs}n"]
                rs = s2rs[(j, si)]
                for i in range(16):
                    for g in range(2):
                        if kind == "p":
                            rhs = Rb[rs][g][:, 2 * i:2 * i + n]
                        else:
                            rhs = Rb[rs][g][:, 32 * i:32 * i + 128]
                        S.add("pe", lambda e, bc=bc, ds=ds, i=i, g=g, rhs=rhs, n=n: e.matmul(
                            ps[64 * g:64 * g + 64, bc, 0:n], lhsT=Wp[ds][:, 16 * g + i, :], rhs=rhs,
                            start=(i == 0), stop=(i == 15), tile_position=(0, 64 * g)),
                            reads=[f"diag{ds}"] + [f"R{rs}_{g}_{jj}" for jj in range(2)], writes=[f"ps{bc}"])
                if si == len(sbs) - 1 and j + 2 < NCH:
                    diag_build(j + 2)
                cres = f"c{j}_{si}"
                S.add("act", lambda e, bc=bc, j=j, off=off, n=n: e.activation(
                    out=cbuf[:, j, off:off + n], in_=ps[:, bc, 0:n], func=AF.Identity, bias=vcol(V_BCONV, j)),
                    reads=[f"ps{bc}", "vecs"], writes=[cres, f"yb{j}_{si}", f"m{j}_{si}"])
                tq = talloc()
                S.add("act", lambda e, bc=bc, j=j, n=n, tq=tq: e.activation(
                    out=tmp[tq][:, 0:n], in_=ps[:, bc, 0:n], func=AF.Square, bias=vcol(V_BCONV, j)),
                    reads=[f"ps{bc}", "vecs"], writes=[f"tmp{tq}"])
                bfree(bc)
                if j == 0:
                    S.add("dve", lambda e, off=off, n=n: e.tensor_copy(out=accs[:, off:off + n], in_=cbuf[:, 0, off:off + n]),
                          reads=[cres], writes=[f"accs{si}"])
                    S.add("dve", lambda e, off=off, n=n, tq=tq: e.tensor_copy(out=accq[:, off:off + n], in_=tmp[tq][:, 0:n]),
                          reads=[f"tmp{tq}"], writes=[f"accq{si}"])
                else:
                    S.add("dve", lambda e, j=j, off=off, n=n: e.tensor_tensor(
                        out=accs[:, off:off + n], in0=accs[:, off:off + n], in1=cbuf[:, j, off:off + n], op=ALU.add),
                        reads=[cres, f"accs{si}"], writes=[f"accs{si}"])
                    S.add("dve", lambda e, off=off, n=n, tq=tq: e.tensor_tensor(
                        out=accq[:, off:off + n], in0=accq[:, off:off + n], in1=tmp[tq][:, 0:n], op=ALU.add),
                        reads=[f"tmp{tq}", f"accq{si}"], writes=[f"accq{si}"])
                nps = sum(1 for _, _, kk in sbs if kk == "p")
                if kind == "p" and si == nps - 1 and not last:
                    S.add("pool", lambda e, vs=vs, j=j, npr=npr: e.tensor_copy(out=halo_v[:, j, :], in_=vext[vs][:, npr:npr + 30]),
                          reads=[f"vext{vs}_{q}" for q in range(nps)], writes=[f"halo_v{j}"])

            def ln_stats():
                for si, (off, n, kind) in enumerate(sbs):
                    b1, b2 = balloc(), balloc()
                    S.add("pe", lambda e, b1=b1, off=off, n=n: e.matmul(ps[:, b1, 0:n], lhsT=onesf[:], rhs=accs[:, off:off + n],
                                                                         start=True, stop=True),
                          reads=["onesf", f"accs{si}"], writes=[f"ps{b1}"])
                    S.add("pe", lambda e, b2=b2, off=off, n=n: e.matmul(ps[:, b2, 0:n], lhsT=onesf[:], rhs=accq[:, off:off + n],
                                                                         start=True, stop=True),
                          reads=["onesf", f"accq{si}"], writes=[f"ps{b2}"])
                    tm, tv = talloc(), talloc()
                    S.add("dve", lambda e, b1=b1, tm=tm, n=n: e.tensor_scalar(
                        out=tmp[tm][:, 0:n], in0=ps[:, b1, 0:n], scalar1=1.0 / D, scalar2=None, op0=ALU.mult),
                        reads=[f"ps{b1}"], writes=[f"tmp{tm}"])
                    S.add("dve", lambda e, tm=tm, tv=tv, n=n: e.tensor_tensor(
                        out=tmp[tv][:, 0:n], in0=tmp[tm][:, 0:n], in1=tmp[tm][:, 0:n], op=ALU.mult),
                        reads=[f"tmp{tm}"], writes=[f"tmp{tv}"])
                    S.add("dve", lambda e, b2=b2, tv=tv, n=n: e.scalar_tensor_tensor(
                        out=tmp[tv][:, 0:n], in0=ps[:, b2, 0:n], scalar=1.0 / D, in1=tmp[tv][:, 0:n],
                        op0=ALU.mult, op1=ALU.subtract),
                        reads=[f"ps{b2}", f"tmp{tv}"], writes=[f"tmp{tv}"])
                    S.add("dve", lambda e, tv=tv, n=n: e.tensor_scalar(
                        out=tmp[tv][:, 0:n], in0=tmp[tv][:, 0:n], scalar1=LN_EPS, scalar2=None, op0=ALU.add),
                        reads=[f"tmp{tv}"], writes=[f"tmp{tv}"])
                    S.add("act", lambda e, tv=tv, n=n: e.activation(out=tmp[tv][:, 0:n], in_=tmp[tv][:, 0:n], func=AF.Ln),
                          reads=[f"tmp{tv}"], writes=[f"tmp{tv}"])
                    S.add("act", lambda e, tv=tv, off=off, n=n: e.activation(
                        out=rstdB[:, off:off + n], in_=tmp[tv][:, 0:n], func=AF.Exp, scale=-0.5),
                        reads=[f"tmp{tv}"], writes=[f"rstdB{si}"])
                    S.add("dve", lambda e, tm=tm, off=off, n=n: e.scalar_tensor_tensor(
                        out=nmrB[:, off:off + n], in0=tmp[tm][:, 0:n], scalar=-1.0, in1=rstdB[:, off:off + n],
                        op0=ALU.mult, op1=ALU.mult),
                        reads=[f"tmp{tm}", f"rstdB{si}"], writes=[f"nmrB{si}"])
                    bfree(b1)
                    bfree(b2)

            def s4_chunk(j):
                wg4, (kz,) = wgroup()
                for si, (off, n, kind) in enumerate(sbs):
                    pz = balloc()
                    for k in range(NCH):
                        S.add("pe", lambda e, pz=pz, kz=kz, k=k, off=off, n=n: e.matmul(
                            ps[:, pz, 0:n], lhsT=ring[kz][:, k, :], rhs=uT[:, k, off:off + n],
                            start=(k == 0), stop=(k == NCH - 1)),
                            reads=[f"ring{kz}"] + ures(off, n), writes=[f"ps{pz}"])
                    if si == len(sbs) - 1:
                        wdone(wg4)
                    tx = talloc()
                    S.add("dve", lambda e, tx=tx, j=j, off=off, n=n: e.tensor_tensor(
                        out=tmp[tx][:, 0:n], in0=cbuf[:, j, off:off + n], in1=rstdB[:, off:off + n], op=ALU.mult),
                        reads=[f"c{j}_{si}", f"rstdB{si}"], writes=[f"tmp{tx}"])
                    S.add("dve", lambda e, tx=tx, off=off, n=n: e.tensor_tensor(
                        out=tmp[tx][:, 0:n], in0=tmp[tx][:, 0:n], in1=nmrB[:, off:off + n], op=ALU.add),
                        reads=[f"tmp{tx}", f"nmrB{si}"], writes=[f"tmp{tx}"])
                    tl = talloc()
                    S.add("act", lambda e, tx=tx, tl=tl, j=j, n=n: e.activation(
                        out=tmp[tl][:, 0:n], in_=tmp[tx][:, 0:n], func=AF.Silu, scale=vcol(V_LNG, j), bias=vcol(V_LNB, j)),
                        reads=[f"tmp{tx}", "vecs"], writes=[f"tmp{tl}"])
                    tb = talloc()
                    S.add("act", lambda e, pz=pz, tb=tb, n=n: e.activation(out=tmp[tb][:, 0:n], in_=ps[:, pz, 0:n], func=AF.Silu),
                          reads=[f"ps{pz}"], writes=[f"tmp{tb}"])
                    S.add("dve", lambda e, tl=tl, tb=tb, j=j, off=off, n=n: e.tensor_tensor(
                        out=cbufb[:, j, 2 * off:2 * off + n], in0=tmp[tl][:, 0:n], in1=tmp[tb][:, 0:n], op=ALU.mult),
                        reads=[f"tmp{tl}", f"tmp{tb}"], writes=[f"yb{j}_{si}", f"c{j}_{si}"])
                    bfree(pz)

            def s3_chunk(j):
                if j == 3:
                    ln_stats()
                wg3, (kc, kh, kb_, kz) = wgroup()
                ss_ = 0
                S.add("pool", lambda e, ss_=ss_, j=j: e.tensor_copy(out=sext[ss_][:, 0:2], in_=halo_s[:, j, :]),
                      reads=[f"halo_s{j}"], writes=[f"sext{ss_}h"])
                if has_s:
                    S.add("pool", lambda e, ss_=ss_, j=j: e.tensor_copy(out=sexts[ss_][:, 0:32], in_=hista[:, j, :]),
                          reads=["hista"], writes=[f"sexts{ss_}h"])
                nps = sum(1 for _, _, kk in sbs if kk == "p")
                for si, (off, n, kind) in enumerate(sbs):
                    bks = [balloc() for _ in range(4)]
                    first = True
                    for (b, kw) in zip(bks, (kc, kh, kb_, kz)):
                        for k in range(NCH):
                            xr = [f"ring{q}" for q in (kc, kh, kb_, kz)] if first else []
                            xw = [f"ps{q}" for q in bks] if first else []
                            first = False
                            S.add("pe", lambda e, b=b, kw=kw, k=k, off=off, n=n: e.matmul(
                                ps[:, b, 0:n], lhsT=ring[kw][:, k, :], rhs=uT[:, k, off:off + n],
                                start=(k == 0), stop=(k == NCH - 1)),
                                reads=[f"ring{kw}"] + ures(off, n) + xr, writes=[f"ps{b}"] + xw)
                    pc, ph, pb, pz = bks
                    if si == len(sbs) - 1:
                        wdone(wg3)
                    th = talloc()
                    S.add("act", lambda e, ph=ph, th=th, n=n: e.activation(out=tmp[th][:, 0:n], in_=ps[:, ph, 0:n], func=AF.Copy),
                          reads=[f"ps{ph}"], writes=[f"tmp{th}"])
                    if kind == "p":
                        sdst, sres = sext[ss_][:, 2 + off:2 + off + n], f"sext{ss_}_{si}"
                        taps = [sext[ss_][:, off + k:off + k + n] for k in range(3)]
                        rres = [f"sext{ss_}h"] + [f"sext{ss_}_{q}" for q in range(si + 1)]
                    else:
                        sdst, sres = sexts[ss_][:, 32:160], f"sexts{ss_}n"
                        taps = [sexts[ss_][:, 16 * k:16 * k + 128] for k in range(3)]
                        rres = [f"sexts{ss_}h", f"sexts{ss_}n"]
                    S.add("dve", lambda e, pc=pc, th=th, n=n, sdst=sdst: e.tensor_tensor(
                        out=sdst, in0=ps[:, pc, 0:n], in1=tmp[th][:, 0:n], op=ALU.mult),
                        reads=[f"ps{pc}", f"tmp{th}"], writes=[sres])
                    tz = talloc()
                    S.add("act", lambda e, pz=pz, tz=tz, n=n: e.activation(out=tmp[tz][:, 0:n], in_=ps[:, pz, 0:n], func=AF.Silu),
                          reads=[f"ps{pz}"], writes=[f"tmp{tz}"])
                    S.add("dve", lambda e, pb=pb, tz=tz, n=n: e.tensor_tensor(
                        out=tmp[tz][:, 0:n], in0=ps[:, pb, 0:n], in1=tmp[tz][:, 0:n], op=ALU.mult),
                        reads=[f"ps{pb}", f"tmp{tz}"], writes=[f"tmp{tz}"])
                    for b in bks:
                        bfree(b)
                    t2 = talloc()
                    S.add("dve", lambda e, t2=t2, taps=taps, n=n, j=j: e.tensor_scalar(
                        out=tmp[t2][:, 0:n], in0=taps[0], scalar1=vecs[:, V_WCA + 3 * j:V_WCA + 3 * j + 1], scalar2=None,
                        op0=ALU.mult), reads=rres + ["vecs"], writes=[f"tmp{t2}"])
                    for k in (1, 2):
                        S.add("dve", lambda e, t2=t2, taps=taps, n=n, j=j, k=k: e.scalar_tensor_tensor(
                            out=tmp[t2][:, 0:n], in0=taps[k], scalar=vecs[:, V_WCA + 3 * j + k:V_WCA + 3 * j + k + 1],
                            in1=tmp[t2][:, 0:n], op0=ALU.mult, op1=ALU.add),
                            reads=rres + ["vecs", f"tmp{t2}"], writes=[f"tmp{t2}"])
                    S.add("dve", lambda e, t2=t2, tz=tz, j=j, off=off, n=n: e.tensor_tensor(
                        out=yaT[:, j, off:off + n], in0=tmp[t2][:, 0:n], in1=tmp[tz][:, 0:n], op=ALU.mult),
                        reads=[f"tmp{t2}", f"tmp{tz}"], writes=[f"ya{j}_{si}"])
                    if last:
                        if kind == "p":
                            S.add("pool", lambda e, ss_=ss_, j=j: e.tensor_copy(out=skeep[:, j, :], in_=sext[ss_][:, 512:514]),
                                  reads=[sres], writes=[f"skeep{j}"])
                        else:
                            S.add("pool", lambda e, ss_=ss_, j=j: e.tensor_copy(out=skeep_s[:, j, :], in_=sexts[ss_][:, 128:160]),
                                  reads=[sres], writes=[f"skeep_s{j}"])
                    if kind == "p" and si == nps - 1 and not last:
                        S.add("pool", lambda e, ss_=ss_, j=j, npr=npr: e.tensor_copy(out=halo_s[:, j, :], in_=sext[ss_][:, npr:npr + 2]),
                              reads=[f"sext{ss_}_{q}" for q in range(nps)], writes=[f"halo_s{j}"])

                if j >= 4:
                    s4_chunk(2 * (j - 4))
                    s4_chunk(2 * (j - 4) + 1)

            SKEW = 2
            for idx in range(len(jobs) + SKEW):
                if idx < len(jobs):
                    s2_proj(*jobs[idx])
                if idx == len(jobs):
                    s3_chunk(0)
                if idx >= SKEW:
                    s2_conv(*jobs[idx - SKEW])
            if bi == 0:
                hist_setup()

            for j in range(1, NCH):
                s3_chunk(j)

            if last:
                out_T(lambda j: vkeep_s[:, j, :], 128, [f"vkeep_s{j}" for j in range(NCH)], BB[0], ["BB0_0", "BB0_1"],
                      [(ncbs_new_d, 0, 128, "o_ncbs")], extra_reads=["BB1_0", "BB1_1"])

            if next_tiles:
                s1_load(next_tiles, 0)
            for ti in range(min(2, len(tiles))):
                row0, off = tiles[ti]
                sl = ti % 2
                S.add("sp", lambda e, sl=sl, row0=row0: e.dma_start(out=xs[sl][:], in_=x_d[row0:row0 + 128, :]),
                      writes=[f"xs{sl}"], dma=f"xs{sl}")
                S.add("pool", lambda e, sl=sl, row0=row0: e.dma_start(out=pbt[sl][:], in_=p_d[row0:row0 + 128, :]),
                      writes=[f"pbt{sl}", swq()], dma=f"pbt{sl}")

            for j in range(NCH):
                wg5, (ka, kbb, kga, kgb) = wgroup()
                for si, (off, n, kind) in enumerate(sbs):
                    bks = [balloc() for _ in range(4)]
                    pya, pyb, pga, pgb = bks
                    for k in range(NCH):
                        xr = [f"ring{q}" for q in (ka, kbb, kga, kgb)] if k == 0 else []
                        xw = [f"ps{q}" for q in bks] if k == 0 else []
                        S.add("pe", lambda e, pya=pya, ka=ka, k=k, off=off, n=n: e.matmul(
                            ps[:, pya, 0:n], lhsT=ring[ka][:, k, :], rhs=yaT[:, k, off:off + n],
                            start=(k == 0), stop=(k == NCH - 1)),
                            reads=[f"ring{ka}", f"ya{k}_{si}"] + xr, writes=[f"ps{pya}"] + xw)
                    for k in range(NCH):
                        S.add("pe", lambda e, pyb=pyb, kbb=kbb, k=k, off=off, n=n: e.matmul(
                            ps[:, pyb, 0:n], lhsT=ring[kbb][:, k, :], rhs=cbufb[:, k, 2 * off:2 * off + n],
                            start=(k == 0), stop=(k == NCH - 1)),
                            reads=[f"ring{kbb}", f"yb{k}_{si}"], writes=[f"ps{pyb}"])
                    for (b, kw) in ((pga, kga), (pgb, kgb)):
                        for k in range(NCH):
                            S.add("pe", lambda e, b=b, kw=kw, k=k, off=off, n=n: e.matmul(
                                ps[:, b, 0:n], lhsT=ring[kw][:, k, :], rhs=uT[:, k, off:off + n],
                                start=(k == 0), stop=(k == NCH - 1)),
                                reads=[f"ring{kw}"] + ures(off, n), writes=[f"ps{b}"])
                    if si == len(sbs) - 1:
                        wdone(wg5)
                    ta, tb = talloc(), talloc()
                    S.add("act", lambda e, pga=pga, ta=ta, n=n: e.activation(out=tmp[ta][:, 0:n], in_=ps[:, pga, 0:n], func=AF.Sigmoid),
                          reads=[f"ps{pga}"], writes=[f"tmp{ta}"])
                    S.add("act", lambda e, pgb=pgb, tb=tb, n=n: e.activation(out=tmp[tb][:, 0:n], in_=ps[:, pgb, 0:n], func=AF.Sigmoid),
                          reads=[f"ps{pgb}"], writes=[f"tmp{tb}"])
                    S.add("dve", lambda e, pya=pya, ta=ta, n=n: e.tensor_tensor(
                        out=tmp[ta][:, 0:n], in0=ps[:, pya, 0:n], in1=tmp[ta][:, 0:n], op=ALU.mult),
                        reads=[f"ps{pya}", f"tmp{ta}"], writes=[f"tmp{ta}"])
                    S.add("dve", lambda e, pyb=pyb, tb=tb, n=n: e.tensor_tensor(
                        out=tmp[tb][:, 0:n], in0=ps[:, pyb, 0:n], in1=tmp[tb][:, 0:n], op=ALU.mult),
                        reads=[f"ps{pyb}", f"tmp{tb}"], writes=[f"tmp{tb}"])
                    S.add("dve", lambda e, ta=ta, tb=tb, j=j, off=off, n=n: e.tensor_tensor(
                        out=cbufb[:, j, 2 * off + n:2 * off + 2 * n], in0=tmp[ta][:, 0:n], in1=tmp[tb][:, 0:n], op=ALU.add),
                        reads=[f"tmp{ta}", f"tmp{tb}"], writes=[f"m{j}_{si}"])
                    for b in bks:
                        bfree(b)

            def sb_of(off):
                for si, (o, n, kind) in enumerate(sbs):
                    if o <= off < o + n:
                        return si, o, n
                raise AssertionError

            s6 = {}

            def s6_xload(ti):
                row0, off = tiles[ti]
                sl = ti % 2
                S.add("sp", lambda e, sl=sl, row0=row0: e.dma_start(out=xs[sl][:], in_=x_d[row0:row0 + 128, :]),
                      writes=[f"xs{sl}"], dma=f"xs{sl}")

            def s6_pload(ti):
                row0, off = tiles[ti]
                sl = ti % 2
                S.add("pool", lambda e, sl=sl, row0=row0: e.dma_start(out=pbt[sl][:], in_=p_d[row0:row0 + 128, :]),
                      writes=[f"pbt{sl}", swq()], dma=f"pbt{sl}")

            def s6_A(ti):
                row0, off = tiles[ti]
                si, o, n = sb_of(off)
                sl = ti % 2
                if ti >= 2:
                    s6_pload(ti)
                mcol = 2 * o + n + (off - o)
                bm = [balloc(), balloc()]
                for h in range(2):
                    for k in range(NCH):
                        xw = [f"ps{bm[1]}"] if (h == 0 and k == 0) else []
                        S.add("pe", lambda e, h=h, k=k, bm=bm, mcol=mcol: e.matmul(
                            ps[:, bm[h], :], lhsT=cbufb[:, k, mcol:mcol + 128], rhs=wo_t[:, k, 512 * h:512 * h + 512],
                            start=(k == 0), stop=(k == NCH - 1)),
                            reads=[f"m{k}_{si}", f"wo{k // 4}"], writes=[f"ps{bm[h]}"] + xw)
                for h in range(2):
                    S.add("dve", lambda e, h=h, sl=sl, bm=bm: e.tensor_tensor(
                        out=Abuf[sl][:, 512 * h:512 * h + 512], in0=ps[:, bm[h], :], in1=xs[sl][:, 512 * h:512 * h + 512],
                        op=ALU.add), reads=[f"ps{bm[h]}", f"xs{sl}"], writes=[f"A{sl}_{h}"])
                bfree(bm[0])
                bfree(bm[1])
                S.add("act", lambda e, sl=sl: e.activation(out=xb[sl][:], in_=Abuf[sl][:], func=AF.Copy),
                      reads=[f"A{sl}_0", f"A{sl}_1"], writes=[f"xb{sl}"])
                s6[ti] = rms_stats(Abuf[sl][:], [f"A{sl}_0", f"A{sl}_1"], xs[sl][:], [f"xs{sl}"])
                if ti + 2 < len(tiles):
                    s6_xload(ti + 2)

            def s6_B(ti):
                sl = ti % 2
                bu = balloc()
                for k in range(NCH):
                    S.add("pe", lambda e, sl=sl, k=k, bu=bu: e.transpose(out=psb[:, bu, k * 128:(k + 1) * 128],
                                                                          in_=xb[sl][:, k * 128:(k + 1) * 128], identity=idb[:]),
                          reads=[f"xb{sl}", "idb"], writes=[f"ps{bu}"])
                S.add("dve", lambda e, bu=bu: e.tensor_tensor(
                    out=u2T[:], in0=psb[:, bu, :].rearrange("p (k t) -> p k t", t=128),
                    in1=vecs[:, V_GPLE:V_GPLE + 8].unsqueeze(2).to_broadcast([128, NCH, 128]), op=ALU.mult),
                    reads=[f"ps{bu}", "vecs"], writes=["u2T"])
                bfree(bu)
                bp = balloc()
                for q in range(2):
                    S.add("pe", lambda e, sl=sl, q=q, bp=bp: e.transpose(out=psb[:, bp, q * 128:(q + 1) * 128],
                                                                          in_=pbt[sl][:, q * 128:(q + 1) * 128], identity=idb[:]),
                          reads=[f"pbt{sl}", "idb"], writes=[f"ps{bp}"])
                S.add("act", lambda e, bp=bp: e.activation(out=pT[:], in_=psb[:, bp, 0:256].rearrange("p (k t) -> p k t", t=128),
                                                           func=AF.Copy),
                      reads=[f"ps{bp}"], writes=["pT"])
                bfree(bp)

            def s6_C(ti):
                row0, off = tiles[ti]
                sl = ti % 2
                bg = [balloc(), balloc()]
                be = [balloc(), balloc()]
                for h in range(2):
                    for k in range(NCH):
                        xw = [f"ps{q}" for q in bg + be] if (h == 0 and k == 0) else []
                        xr = ["pT", "wpe"] if (h == 0 and k == 0) else []
                        S.add("pe", lambda e, h=h, k=k, bg=bg: e.matmul(
                            ps[:, bg[h], :], lhsT=u2T[:, k, :], rhs=wpg_t[:, k, 512 * h:512 * h + 512],
                            start=(k == 0), stop=(k == NCH - 1)),
                            reads=["u2T", f"wpg{k // 4}"] + xr, writes=[f"ps{bg[h]}"] + xw)
                for h in range(2):
                    for q in range(2):
                        S.add("pe", lambda e, h=h, q=q, be=be: e.matmul(
                            ps[:, be[h], :], lhsT=pT[:, q, :], rhs=wpe_t[:, q, 512 * h:512 * h + 512],
                            start=(q == 0), stop=(q == 1)),
                            reads=["pT", "wpe"], writes=[f"ps{be[h]}"])
                rs2_ap, rs2_res = s6[ti]
                bs = ti % 2
                Bt = BB[bs]
                for h in range(2):
                    S.add("act", lambda e, h=h, bg=bg, rs2_ap=rs2_ap, Bt=Bt: e.activation(
                        out=Bt[:, 512 * h:512 * h + 512], in_=ps[:, bg[h], :], func=AF.Sigmoid, scale=rs2_ap),
                        reads=[f"ps{bg[h]}", rs2_res], writes=[f"BB{bs}_{h}"])
                    S.add("dve", lambda e, h=h, be=be, Bt=Bt: e.tensor_tensor(
                        out=Bt[:, 512 * h:512 * h + 512], in0=ps[:, be[h], :], in1=Bt[:, 512 * h:512 * h + 512], op=ALU.mult),
                        reads=[f"ps{be[h]}", f"BB{bs}_{h}"], writes=[f"BB{bs}_{h}"])
                for b in bg + be:
                    bfree(b)
                S.add("dve", lambda e, sl=sl, Bt=Bt: e.tensor_tensor(out=Bt, in0=Bt, in1=Abuf[sl][:], op=ALU.add),
                      reads=[f"BB{bs}_0", f"BB{bs}_1", f"A{sl}_0", f"A{sl}_1"], writes=[f"BB{bs}_0", f"BB{bs}_1"])

            def s6_C2(ti):
                pump(2)
                row0, off = tiles[ti]
                bs = ti % 2
                Bt = BB[bs]
                rs_ap, rs_res = rms_stats(Bt, [f"BB{bs}_0", f"BB{bs}_1"], yt[:], ["yt"])
                S.add("dve", lambda e, rs_ap=rs_ap, Bt=Bt: e.scalar_tensor_tensor(
                    out=yt[:], in0=Bt, scalar=rs_ap, in1=gfin[:], op0=ALU.mult, op1=ALU.mult),
                    reads=[f"BB{bs}_0", f"BB{bs}_1", rs_res, "gfin"], writes=["yt"])
                S.add("sp", lambda e, row0=row0: e.dma_start(out=y_d[row0:row0 + 128, :], in_=yt[:]),
                      reads=["yt"], dma="yt")

            nt = len(tiles)
            s6_A(0)
            if nt > 1:
                s6_A(1)
            s6_B(0)
            for ti in range(nt):
                h2 = bool(next_tiles) and nt >= 3 and ti == nt - 2
                h1 = bool(next_tiles) and nt >= 3 and ti == nt - 1
                if h2:
                    s1_stats(next_tiles, 0)
                if h1:
                    s1_stats(next_tiles, 1)
                    s1_stats(next_tiles, 2)
                s6_C(ti)
                if h1:
                    s1_rest(next_tiles, 1)
                    if len(next_tiles) > 4:
                        s1_load(next_tiles, 4)
                if ti + 1 < nt:
                    s6_B(ti + 1)
                if h2:
                    s1_rest(next_tiles, 0)
                    if len(next_tiles) > 3:
                        s1_load(next_tiles, 3)
                if ti + 2 < nt:
                    s6_A(ti + 2)
                    if ti + 2 == nt - 1 and next_tiles:
                        s1_load(next_tiles, 1)
                        s1_load(next_tiles, 2)
                s6_C2(ti)

        out_T(lambda j: vkeep[:, j, :], 32, [f"vkeep{j}" for j in range(NCH)], Abuf[0], ["A0_0", "A0_1"],
              [(ncbp_d, 2, 32, "o_ncbp")])
        out_T(lambda j: skeep[:, j, :], 2, [f"skeep{j}" for j in range(NCH)], Abuf[1], ["A1_0", "A1_1"],
              [(ncap_d, 0, 2, "o_ncap")])
        out_T(lambda j: skeep_s[:, j, :], 32, [f"skeep_s{j}" for j in range(NCH)], xs[0], ["xs0", "xs0"],
              [(ncas_d, 0, 32, "o_ncas")])

        S.emit()
    return nc


_CACHE = {}


def kernel(x_prompt, x_sample, state_conv_a, state_conv_b, p_prompt, p_sample,
           g_mix, w_in, w_conv_a, w_out_a, w_conv_b, b_conv_b, ln_g, ln_b,
           w_out_b, w_o, w_pe, g_ple, w_pg, g_final):
    f = lambda a: np.ascontiguousarray(np.asarray(a, dtype=np.float32))
    x_prompt, x_sample = f(x_prompt), f(x_sample)
    state_conv_a, state_conv_b = f(state_conv_a), f(state_conv_b)
    p_prompt, p_sample = f(p_prompt), f(p_sample)
    NC = 8
    vecs = np.zeros((128, NV), np.float32)

    def fm(v):
        return f(v).reshape(8, 128).T

    vecs[:, V_GMIX:V_GMIX + 8] = fm(g_mix[0])
    vecs[:, V_GPLE:V_GPLE + 8] = fm(g_ple[0])
    vecs[:, V_BCONV:V_BCONV + 8] = fm(b_conv_b[0])
    vecs[:, V_LNG:V_LNG + 8] = fm(ln_g[0])
    vecs[:, V_LNB:V_LNB + 8] = fm(ln_b[0])
    wca = f(w_conv_a[0])
    vecs[:, V_WCA:V_WCA + 24] = wca.reshape(3, 8, 128).transpose(2, 1, 0).reshape(128, 24)
    wcb = f(w_conv_b[0])
    wpad = np.concatenate([wcb, np.zeros((1, D), np.float32)], axis=0)
    wl = wpad.reshape(16, 2, 8, 2, 64).transpose(1, 4, 2, 3, 0)
    vecs[:, V_WCB:V_WCB + 256] = wl.reshape(128, 256)
    vecs[:, V_MASK:V_MASK + 64] = np.tile(np.eye(64, dtype=np.float32), (2, 1))
    gfin = np.ascontiguousarray(np.broadcast_to(f(g_final)[None, :], (128, D)))
    ident = np.eye(128, dtype=np.float32)
    shared = dict(w_in=f(w_in[0]), w_oa=f(w_out_a[0]), w_ob=f(w_out_b[0]), w_o=f(w_o[0]), w_pg=f(w_pg[0]),
                  w_pe=f(w_pe[0]), vecs=vecs, gfin=gfin, ident=ident)
    in_maps = []
    for c in range(NC):
        s0 = 16 * c
        xs_ = x_sample[s0:s0 + 16].transpose(1, 0, 2).reshape(128, D)
        ps_ = p_sample[0, s0:s0 + 16].transpose(1, 0, 2).reshape(128, 256)
        m = dict(shared)
        m["x"] = np.ascontiguousarray(np.concatenate([x_prompt[c], xs_], axis=0))
        m["p"] = np.ascontiguousarray(np.concatenate([p_prompt[0, c], ps_], axis=0))
        m["sa"] = np.ascontiguousarray(state_conv_a[0, s0:s0 + 16].transpose(1, 0, 2).reshape(32, D))
        m["sb"] = np.ascontiguousarray(state_conv_b[0, s0:s0 + 16].transpose(1, 0, 2).reshape(480, D))
        in_maps.append(m)
    if "nc" not in _CACHE:
        _CACHE["nc"] = build_program()
    res = run_bass_kernel_spmd(_CACHE["nc"], in_maps, core_ids=list(range(NC)))
    R = res.results
    y_prompt = np.stack([R[c]["y"][:2048] for c in range(NC)], axis=0)
    y_sample = np.concatenate(
        [R[c]["y"][2048:].reshape(8, 16, D).transpose(1, 0, 2) for c in range(NC)], axis=0)
    nca_p = np.stack([R[c]["nca_p"] for c in range(NC)], axis=0)[None]
    ncb_p = np.stack([R[c]["ncb_p"] for c in range(NC)], axis=0)[None]
    nca_s = np.concatenate([R[c]["nca_s"].reshape(2, 16, D).transpose(1, 0, 2) for c in range(NC)], axis=0)[None]
    ncb_s = np.concatenate(
        [np.concatenate([R[c]["ncb_s_hist"].reshape(22, 16, D).transpose(1, 0, 2),
                         R[c]["ncb_s_new"].reshape(8, 16, D).transpose(1, 0, 2)], axis=1) for c in range(NC)],
        axis=0)[None]
    out = (y_prompt, y_sample, nca_p, ncb_p, nca_s, ncb_s)
    return tuple(np.ascontiguousarray(o, dtype=np.float32) for o in out)
```
